# Optimizing a Trainium2 kernel written in Bass

```python
import math
import jax, jax.numpy as jnp
from jax import lax
import numpy as np

D_MODEL = 2048
BATCH = 4
SEQ = 2048
DEPTH = 2
DEC_BATCH = 128
DEC_SEQ = 4
PAST_LEN = 8192
PAGE_SIZE = 128

N_A_LAYERS = DEPTH // 2
N_B_LAYERS = DEPTH - N_A_LAYERS
HGRN_EXPAND = 128
HGRN_HEADS = D_MODEL // HGRN_EXPAND
HGRN_DK = HGRN_EXPAND
HGRN_DV = D_MODEL // HGRN_HEADS
HGRN_WIDTH = HGRN_HEADS * HGRN_DK
HGRN_CHUNK = 16
ATTN_HEAD_DIM = 64
ATTN_Q_HEADS = D_MODEL // ATTN_HEAD_DIM
ATTN_KV_HEADS = max(1, ATTN_Q_HEADS // 8)
ATTN_GROUP = ATTN_Q_HEADS // ATTN_KV_HEADS
WINDOW = 128
ROT_DIM = ATTN_HEAD_DIM // 4
ROPE_THETA = 500000.0
D_FF = int(math.ceil(8 * D_MODEL / 3 / 256)) * 256
RMS_EPS = 1e-6

kernel_name = "yoco_hgrn2_swa_sink_decode_step"


def rmsnorm(x, g):
    xf = x.astype(jnp.float32)
    y = xf * lax.rsqrt(jnp.mean(xf * xf, axis=-1, keepdims=True) + RMS_EPS)
    return (y * g.astype(jnp.float32)).astype(x.dtype)


def swiglu(xn, w_gate_up, w_down):
    gate, up = jnp.split(xn @ w_gate_up, 2, axis=-1)
    return (jax.nn.silu(gate) * up) @ w_down


def rope(x, pos):
    inv = ROPE_THETA ** (-jnp.arange(0, ROT_DIM, 2, dtype=jnp.float32) / ROT_DIM)
    ang = pos.astype(jnp.float32)[:, None] * inv[None, :]
    cos = jnp.cos(ang)[None, :, None, :]
    sin = jnp.sin(ang)[None, :, None, :]
    xr = x[..., :ROT_DIM].astype(jnp.float32)
    x1, x2 = jnp.split(xr, 2, axis=-1)
    rot = jnp.concatenate([x1 * cos - x2 * sin, x2 * cos + x1 * sin], axis=-1)
    return jnp.concatenate([rot.astype(x.dtype), x[..., ROT_DIM:]], axis=-1)


def gla_chunked(q, k, v, log_f, S0):
    B, T, H, DK = q.shape
    DV = v.shape[-1]
    C = math.gcd(T, HGRN_CHUNK)
    n = T // C

    def to_chunks(a):
        return jnp.moveaxis(a.astype(jnp.float32).reshape(B, n, C, *a.shape[2:]), 1, 0)

    tril = jnp.tril(jnp.ones((C, C), dtype=bool))

    def step(S, inp):
        qc, kc, vc, gc = inp
        b = jnp.cumsum(gc, axis=1)
        o_inter = jnp.einsum('bthk,bhkv->bthv', qc * jnp.exp(b), S)
        diff = b[:, :, None] - b[:, None, :]
        decay = jnp.exp(jnp.where(tril[None, :, :, None, None], diff, -jnp.inf))
        scores = jnp.einsum('bthk,bshk,btshk->btsh', qc, kc, decay)
        o_intra = jnp.einsum('btsh,bshv->bthv', scores, vc)
        b_last = b[:, -1]
        S_new = jnp.exp(b_last)[..., None] * S + jnp.einsum(
            'bshk,bshv->bhkv', kc * jnp.exp(b_last[:, None] - b), vc)
        return S_new, o_inter + o_intra

    S, o = lax.scan(step, S0.astype(jnp.float32),
                    (to_chunks(q), to_chunks(k), to_chunks(v), to_chunks(log_f)))
    o = jnp.moveaxis(o, 0, 1).reshape(B, T, H, DV)
    return o.astype(q.dtype), S.astype(S0.dtype)


def hgrn2_mixer(xn, w_in, lb, g_norm, w_out, S0):
    B, T, _ = xn.shape
    q, f, i, g = jnp.split(xn @ w_in, 4, axis=-1)
    q = jax.nn.silu(q).reshape(B, T, HGRN_HEADS, HGRN_DK)
    fg = lb + (1.0 - lb) * jax.nn.sigmoid(f.astype(jnp.float32))
    log_f = jnp.log(fg).reshape(B, T, HGRN_HEADS, HGRN_DK)
    k = (1.0 - fg).astype(xn.dtype).reshape(B, T, HGRN_HEADS, HGRN_DK)
    v = i.reshape(B, T, HGRN_HEADS, HGRN_DV)
    o, S = gla_chunked(q, k, v, log_f, S0)
    o = rmsnorm(o, g_norm.reshape(HGRN_HEADS, HGRN_DV))
    o = o.reshape(B, T, HGRN_WIDTH) * jax.nn.silu(g)
    return o @ w_out, S


def shared_kv(h, kv_norm, w_kv, pos):
    B, T, _ = h.shape
    k, v = jnp.split(rmsnorm(h, kv_norm) @ w_kv, 2, axis=-1)
    k = rope(k.reshape(B, T, ATTN_KV_HEADS, ATTN_HEAD_DIM), pos)
    v = v.reshape(B, T, ATTN_KV_HEADS, ATTN_HEAD_DIM)
    return k, v


def sink_attention(q, k, v, q_pos, k_pos, sinks):
    s = jnp.einsum('bnqhgd,bnkhd->bnhgqk', q, k).astype(jnp.float32) * (ATTN_HEAD_DIM ** -0.5)
    qp = q_pos[:, :, None]
    kp = k_pos[:, None, :]
    mask = (kp >= 0) & (kp <= qp) & (qp - kp < WINDOW)
    s = jnp.where(mask[None, :, None, None], s, -jnp.inf)
    sink = sinks.astype(jnp.float32).reshape(ATTN_KV_HEADS, ATTN_GROUP)[None, None, :, :, None, None]
    m = jnp.maximum(jnp.max(s, axis=-1, keepdims=True), sink)
    p = jnp.exp(s - m)
    p = p / (jnp.sum(p, axis=-1, keepdims=True) + jnp.exp(sink - m))
    return jnp.einsum('bnhgqk,bnkhd->bnqhgd', p.astype(v.dtype), v)


def swa_prompt(xn, w_q, sinks, w_out, k, v, pos):
    B, T, _ = xn.shape
    BLK = WINDOW
    nb = T // BLK
    q = rope((xn @ w_q).reshape(B, T, ATTN_Q_HEADS, ATTN_HEAD_DIM), pos)
    q = q.reshape(B, nb, BLK, ATTN_KV_HEADS, ATTN_GROUP, ATTN_HEAD_DIM)
    pad = ((0, 0), (BLK, 0), (0, 0), (0, 0))
    k_pad, v_pad = jnp.pad(k, pad), jnp.pad(v, pad)

    def band(a):
        return jnp.concatenate([a[:, :T].reshape(B, nb, BLK, *a.shape[2:]),
                                a[:, BLK:].reshape(B, nb, BLK, *a.shape[2:])], axis=2)

    kp_pad = jnp.arange(-BLK, T, dtype=jnp.int32)
    kp = jnp.concatenate([kp_pad[:T].reshape(nb, BLK), kp_pad[BLK:].reshape(nb, BLK)], axis=1)
    o = sink_attention(q, band(k_pad), band(v_pad), pos.reshape(nb, BLK), kp, sinks)
    return o.reshape(B, T, ATTN_Q_HEADS * ATTN_HEAD_DIM) @ w_out


def swa_sample(xn, w_q, sinks, w_out, k_all, v_all, q_pos, k_pos):
    B, T, _ = xn.shape
    q = rope((xn @ w_q).reshape(B, T, ATTN_Q_HEADS, ATTN_HEAD_DIM), q_pos)
    q = q.reshape(B, 1, T, ATTN_KV_HEADS, ATTN_GROUP, ATTN_HEAD_DIM)
    o = sink_attention(q, k_all[:, None], v_all[:, None], q_pos[None], k_pos[None], sinks)
    return o.reshape(B, T, ATTN_Q_HEADS * ATTN_HEAD_DIM) @ w_out


def trunk(x, pos, S_init, past_k, past_v, past_pos,
          norm_mix_pre, norm_mix_post, norm_ffn_pre, norm_ffn_post,
          hgrn_w_in, hgrn_lower_bounds, hgrn_g_norm, hgrn_w_out,
          kv_norm, w_kv, attn_w_q, attn_sinks, attn_w_out,
          ffn_w_gate_up, ffn_w_down):
    lbs = jnp.cumsum(jax.nn.softmax(hgrn_lower_bounds.astype(jnp.float32), axis=0), axis=0)
    h = x
    new_states = []
    k_sh = v_sh = None
    for l in range(DEPTH):
        hn = rmsnorm(h, norm_mix_pre[l])
        if l < N_A_LAYERS:
            mix, S_l = hgrn2_mixer(hn, hgrn_w_in[l], lbs[l], hgrn_g_norm[l], hgrn_w_out[l], S_init[l])
            new_states.append(S_l)
        else:
            j = l - N_A_LAYERS
            if past_k is None:
                mix = swa_prompt(hn, attn_w_q[j], attn_sinks[j], attn_w_out[j], k_sh, v_sh, pos)
            else:
                mix = swa_sample(hn, attn_w_q[j], attn_sinks[j], attn_w_out[j],
                                 jnp.concatenate([past_k, k_sh], axis=1),
                                 jnp.concatenate([past_v, v_sh], axis=1),
                                 pos, jnp.concatenate([past_pos, pos]))
        h = h + rmsnorm(mix, norm_mix_post[l])
        h = h + rmsnorm(swiglu(rmsnorm(h, norm_ffn_pre[l]), ffn_w_gate_up[l], ffn_w_down[l]),
                        norm_ffn_post[l])
        if l == N_A_LAYERS - 1:
            k_sh, v_sh = shared_kv(h, kv_norm, w_kv, pos)
    return h, jnp.stack(new_states), k_sh, v_sh


def setup_inputs(seed: int = 0) -> dict:
    key = jax.random.key(seed)
    ks = jax.random.split(key, 24)
    f32 = jnp.float32

    def w(k, shape, fan_in):
        return jax.random.normal(k, shape, f32) * (fan_in ** -0.5)

    def gain(k, shape):
        return 1.0 + 0.02 * jax.random.normal(k, shape, f32)

    w_buf = min(WINDOW, PAST_LEN)
    return {
        "x_prompt": jax.random.normal(ks[0], (BATCH, SEQ, D_MODEL), f32),
        "x_sample": jax.random.normal(ks[1], (DEC_BATCH, DEC_SEQ, D_MODEL), f32),
        "state_hgrn": jax.random.normal(ks[2], (N_A_LAYERS, DEC_BATCH, HGRN_HEADS, HGRN_DK, HGRN_DV), f32),
        "cache_k_win": jax.random.normal(ks[3], (DEC_BATCH, w_buf, ATTN_KV_HEADS, ATTN_HEAD_DIM), f32),
        "cache_v_win": jax.random.normal(ks[4], (DEC_BATCH, w_buf, ATTN_KV_HEADS, ATTN_HEAD_DIM), f32),
        "norm_mix_pre": gain(ks[5], (DEPTH, D_MODEL)),
        "norm_mix_post": gain(ks[6], (DEPTH, D_MODEL)),
        "norm_ffn_pre": gain(ks[7], (DEPTH, D_MODEL)),
        "norm_ffn_post": gain(ks[8], (DEPTH, D_MODEL)),
        "hgrn_w_in": w(ks[9], (N_A_LAYERS, D_MODEL, 4 * HGRN_WIDTH), D_MODEL),
        "hgrn_lower_bounds": jax.random.normal(ks[10], (N_A_LAYERS + 1, HGRN_WIDTH), f32),
        "hgrn_g_norm": gain(ks[11], (N_A_LAYERS, HGRN_HEADS * HGRN_DV)),
        "hgrn_w_out": w(ks[12], (N_A_LAYERS, HGRN_WIDTH, D_MODEL), HGRN_WIDTH),
        "kv_norm": gain(ks[13], (D_MODEL,)),
        "w_kv": w(ks[14], (D_MODEL, 2 * ATTN_KV_HEADS * ATTN_HEAD_DIM), D_MODEL),
        "attn_w_q": w(ks[15], (N_B_LAYERS, D_MODEL, ATTN_Q_HEADS * ATTN_HEAD_DIM), D_MODEL),
        "attn_sinks": 0.5 * jax.random.normal(ks[16], (N_B_LAYERS, ATTN_Q_HEADS), f32),
        "attn_w_out": w(ks[17], (N_B_LAYERS, ATTN_Q_HEADS * ATTN_HEAD_DIM, D_MODEL), ATTN_Q_HEADS * ATTN_HEAD_DIM),
        "ffn_w_gate_up": w(ks[18], (DEPTH, D_MODEL, 2 * D_FF), D_MODEL),
        "ffn_w_down": w(ks[19], (DEPTH, D_FF, D_MODEL), D_FF),
    }


def reference(x_prompt, x_sample, state_hgrn, cache_k_win, cache_v_win,
              norm_mix_pre, norm_mix_post, norm_ffn_pre, norm_ffn_post,
              hgrn_w_in, hgrn_lower_bounds, hgrn_g_norm, hgrn_w_out,
              kv_norm, w_kv, attn_w_q, attn_sinks, attn_w_out,
              ffn_w_gate_up, ffn_w_down):
    weights = (norm_mix_pre, norm_mix_post, norm_ffn_pre, norm_ffn_post,
               hgrn_w_in, hgrn_lower_bounds, hgrn_g_norm, hgrn_w_out,
               kv_norm, w_kv, attn_w_q, attn_sinks, attn_w_out,
               ffn_w_gate_up, ffn_w_down)
    Bp, Tp, _ = x_prompt.shape
    Ts = x_sample.shape[1]
    pos_p = jnp.arange(Tp, dtype=jnp.int32)
    S0_p = jnp.zeros((N_A_LAYERS, Bp, HGRN_HEADS, HGRN_DK, HGRN_DV), x_prompt.dtype)
    y_prompt, S_p, k_p, v_p = trunk(x_prompt, pos_p, S0_p, None, None, None, *weights)
    w_buf = cache_k_win.shape[1]
    pos_s = PAST_LEN + jnp.arange(Ts, dtype=jnp.int32)
    past_pos = PAST_LEN - w_buf + jnp.arange(w_buf, dtype=jnp.int32)
    y_sample, S_s, k_s, v_s = trunk(x_sample, pos_s, state_hgrn, cache_k_win, cache_v_win, past_pos, *weights)
    w_p = min(WINDOW, Tp)
    return (y_prompt, y_sample, S_p, S_s, k_p[:, Tp - w_p:], v_p[:, Tp - w_p:], k_s, v_s)
```

```python
import numpy as np
from contextlib import ExitStack
import concourse.bass as bass
import concourse.mybir as mybir
from concourse.bass_utils import run_bass_kernel_spmd

F32 = mybir.dt.float32
BF16 = mybir.dt.bfloat16
U8 = mybir.dt.uint8
AF = mybir.ActivationFunctionType
ALU = mybir.AluOpType

D = 2048
NCH = 16
DFF = 5632
NJ = 44
EPS = 1e-6
NEG = -30000.0
NP_, NA_, NB_ = 896, 672, 544
ENG = ["pe", "act", "dve", "pool", "sp"]
KSPLIT = 2
NREC_RND, NPROJ_RND = 1, 1


def tiles_of(n):
    if n <= 512:
        return [(0, n)]
    h = n // 2
    return [(0, h), (h, n - h)]


class Prog:
    def __init__(self, nc, es):
        self.nc = nc
        self.es = es
        self.ops = {e: [] for e in ENG}
        self.cnt = {e: 0 for e in ENG}
        self.sem = {e: es.enter_context(nc.semaphore("sem_" + e)) for e in ENG}
        self.waited = {e: {} for e in ENG}
        self.lastw = {}
        self.readers = {}
        self.chan = {}

    def _deps(self, eng, reads, writes):
        deps = []
        for r in list(reads) + list(writes):
            t = self.lastw.get(r)
            if t is not None:
                deps.append(t)
        for r in writes:
            for t in self.readers.get(r, {}).values():
                deps.append(t)
        if eng == "pe":
            deps = [t for t in deps if t[0] != "pe"]
        return deps

    def _mkwaits(self, eng, deps):
        best = {}
        for k, v in deps:
            best[k] = max(best.get(k, 0), v)
        out = []
        for k, v in best.items():
            if self.waited[eng].get(k, 0) >= v:
                continue
            self.waited[eng][k] = v
            out.append((k, v))
        return out

    def _commit(self, tok, key, reads, writes):
        for r in writes:
            self.lastw[r] = tok
            self.readers[r] = {}
        for r in reads:
            if r in writes:
                continue
            self.readers.setdefault(r, {})[key] = tok

    def op(self, eng, fn, reads=(), writes=()):
        writes = list(writes) + [r for r in reads if isinstance(r, tuple) and r[0] == "ps" and r not in writes]
        deps = self._deps(eng, reads, writes)
        waits = self._mkwaits(eng, deps)
        self.cnt[eng] += 1
        tok = (eng, self.cnt[eng])
        self.ops[eng].append(("c", waits, fn))
        self._commit(tok, eng, reads, writes)
        return tok

    def dma(self, q, chan, out, in_, reads=(), writes=()):
        if chan not in self.chan:
            self.chan[chan] = [self.es.enter_context(self.nc.semaphore("ch_" + chan)), 0]
        ch = self.chan[chan]
        deps = self._deps(q, reads, writes)
        key = ("ch", chan)
        if ch[1] > 0:
            deps.append((key, ch[1]))
        waits = self._mkwaits(q, deps)
        ch[1] += 16
        tok = (key, ch[1])
        self.ops[q].append(("d", waits, (lambda e, o=out, i=in_: e.dma_start(out=o, in_=i)), ch[0]))
        self._commit(tok, key, reads, writes)
        return tok

    def fence(self, pool=True):
        allw = [(e, self.cnt[e]) for e in ENG if self.cnt[e] > 0]
        allw += [(("ch", c), v[1]) for c, v in self.chan.items() if v[1] > 0]
        for e in ENG:
            if e == "pool" and not pool:
                continue
            w = self._mkwaits(e, allw)
            if w:
                self.ops[e].append(("w", w))
        keep = lambda r: isinstance(r, tuple) and r[0] == "w"
        self.lastw = {r: t for r, t in self.lastw.items() if keep(r)}
        self.readers = {r: t for r, t in self.readers.items() if keep(r)}

    def _semof(self, k):
        if isinstance(k, tuple):
            return self.chan[k[1]][0]
        return self.sem[k]

    def emit(self):
        nc = self.nc
        self.fence()
        names = {"pe": "tensor", "act": "scalar", "dve": "vector", "pool": "gpsimd", "sp": "sync"}
        with nc.Block() as block:
            for e in ENG:
                def body(eng, e=e):
                    for item in self.ops[e]:
                        for k, v in item[1]:
                            eng.wait_ge(self._semof(k), v)
                        if item[0] == "c":
                            ins = item[2](eng)
                            ins.then_inc(self.sem[e], 1)
                        elif item[0] == "d":
                            item[2](eng).then_inc(item[3], 16)
                getattr(block, names[e])(body)


class Region:
    def __init__(self, nc, es, name, nbytes, parent=None, base=0):
        self.t = parent.t if parent is not None else es.enter_context(nc.sbuf_tensor(name, [128, nbytes], U8))
        self.n = nbytes
        self.off = 0
        self.base = base

    def sub(self, nbytes):
        assert self.off + nbytes <= self.n
        r = Region(None, None, None, nbytes, parent=self, base=self.base + self.off)
        self.off += nbytes
        return r

    def reset(self):
        self.off = 0

    def take(self, dtype, shape):
        esz = 4 if dtype == F32 else 2
        n = int(np.prod(shape)) * esz
        assert self.off + n <= self.n, (self.off, n, self.n)
        ap = self.t[:, self.base + self.off:self.base + self.off + n].bitcast(dtype)
        self.off += n
        if len(shape) == 2:
            return ap.rearrange("p (a b) -> p a b", a=shape[0])
        if len(shape) == 3:
            return ap.rearrange("p (a b c) -> p a b c", a=shape[0], b=shape[1])
        return ap


class StopBuild(Exception):
    pass


def build_program(stop=None):
    def stage_done(nm):
        if stop == nm:
            raise StopBuild()
    nc = bass.Bass("TRN2", target_bir_lowering=False)
    es = ExitStack()
    P = Prog(nc, es)

    def din(name, shape):
        return nc.dram_tensor(name, list(shape), F32, kind="ExternalInput").ap()

    def dout(name, shape):
        return nc.dram_tensor(name, list(shape), F32, kind="ExternalOutput").ap()

    xP = din("xP", [D, NP_]); xA = din("xA", [D, NA_]); xB = din("xB", [D, NB_])
    st_in = din("st_in", [16, 16, 128, 128])
    ckT = din("ckT", [16, 128, 4, 128])
    cv = din("cv", [16, 128, 4, 128])
    w_in = din("w_in", [16, D, 512])
    w_out = din("w_out", [D, D])
    w_gu = din("w_gu", [2, D, NJ * 256])
    w_dn = din("w_dn", [2, DFF, D])
    w_kv = din("w_kv", [D, 1024])
    w_q = din("w_q", [D, D]); w_o = din("w_o", [D, D])
    vecs_d = din("vecs", [128, 9 * 16])
    lbraw_d = din("lbraw", [128, 32])
    gnorm_d = din("gnorm", [128, 16])
    sinks_d = din("sinks", [128, 32])
    cF_d = din("cF", [128, 384])
    NCB = 128 * 7 + 32 + 8 + 32 + 256
    cB_d = din("cB", [128, NCB])
    scan_d = {"P": din("scanP", [128, NP_]), "A": din("scanA", [128, NA_]), "B": din("scanB", [128, NB_])}
    rope_d = {"A": din("ropeA", [128, 2, NA_]), "B": din("ropeB", [128, 2, NB_])}

    yT_o = {"A": dout("yT_A", [D, NA_]), "B": dout("yT_B", [D, NB_])}
    st_out = dout("st_out", [16, 16, 128, 128])
    sp_out = dout("sp_out", [16, 128, 128])
    kv_o = {"A": dout("kv_A", [2, 4, 64, 160]), "B": dout("kv_B", [2, 4, 64, 160])}

    RH = Region(nc, es, "RH", 43008)
    RXU = Region(nc, es, "RXU", 43008)
    RY = Region(nc, es, "RY", 43008)
    WS = Region(nc, es, "WS", 32768)
    RM = Region(nc, es, "RM", 51000)
    wslot = [WS.take(BF16, [16, 512]) for _ in range(2)]
    S_all = RM.take(F32, [16, 128])
    vecs = RM.take(F32, [9, 16])
    lbraw = RM.take(F32, [2, 16])
    lb = RM.take(F32, [1, 16])[:, 0, :]
    oml = RM.take(F32, [1, 16])[:, 0, :]
    gnorm = RM.take(F32, [1, 16])[:, 0, :]
    esink = RM.take(F32, [1, 32])[:, 0, :]
    cF = RM.take(F32, [3, 128])
    identF, onesF, permF = cF[:, 0, :], cF[:, 1, :], cF[:, 2, :]
    cB = RM.take(BF16, [1, NCB])[:, 0, :]
    identB = cB[:, 0:128]; onesB = cB[:, 128:256]; maskBD = cB[:, 256:384]
    Mdiag = cB[:, 384:512]; Mprev = cB[:, 512:640]; MprevF = cB[:, 640:768]
    Mc = cB[:, 768:800]; blockmask = cB[:, 800:808]; maskS = cB[:, 808:840]; Mnew = cB[:, 840:1096]
    zcol = cB[:, 1096:1224]
    scanm = RM.take(F32, [1, NP_])[:, 0, :]
    _r = RM.take(F32, [1, NA_])[:, 0, :]
    rstd = [_r, _r]
    sqt = [RM.take(F32, [1, 512])[:, 0, :] for _ in range(2)]
    sqb = [sqt[i].bitcast(BF16)[:, 0:512] for i in range(2)]
    AL = RM.sub(12288)
    S0b = AL.take(BF16, [8, 128])
    Vblk = AL.take(BF16, [8, 128])
    Ue = [AL.take(F32, [2, 128]) for _ in range(2)]
    Sf = [AL.take(F32, [2, 128]) for _ in range(2)]
    Sb = [AL.take(BF16, [4, 128]) for _ in range(2)]
    AL.reset()
    PT = [AL.take(BF16, [2, 512]) for _ in range(2)]
    KcT = [AL.take(BF16, [4, 2, 128]) for _ in range(2)]
    ATm = [RM.take(BF16, [1, 128])[:, 0, :] for _ in range(2)]
    KTc = RM.take(BF16, [4, 2, 128])
    Vc_carry = RM.take(BF16, [4, 128])
    epsc = RM.take(F32, [1, 8])[:, 0, :]
    kvst = RM.take(F32, [2, 160])
    nrm = [RM.take(F32, [1, 512])[:, 0, :] for _ in range(2)]
    ropeT = RM.take(F32, [2, NA_])

    PS = es.enter_context(nc.psum_tensor("PS", [128, 4096], F32))

    def bank(i, n=512):
        return PS[:, i * 512:i * 512 + n]

    def bankb(i):
        return PS[:, i * 512:(i + 1) * 512].bitcast(BF16)

    wstate = {"i": 0}

    def load_w(src_ap, nkc, ncols):
        s = wstate["i"] % 2
        wstate["i"] += 1
        P.dma("pool", "w%d" % s, wslot[s][:, 0:nkc, 0:ncols], src_ap.rearrange("(kc p) n -> p kc n", p=128),
              writes=[("w", s)])
        return s

    pstate = {"i": 0}

    def next_pbank():
        b = pstate["i"] % 3
        pstate["i"] += 1
        return b

    def linear(src2d, nkc, ncols, in_ap, in_res, N, consume, mi_order=None, tl=None):
        for _ in linear_g(src2d, nkc, ncols, in_ap, in_res, N, consume, mi_order=mi_order, tl=tl):
            pass

    def linear_g(src2d, nkc, ncols, in_ap, in_res, N, consume, mi_order=None, tl=None, between=None, ksplit=1):
        s = load_w(src2d, nkc, ncols)
        uidx = 0
        tl = tl or tiles_of(N)
        order = mi_order if mi_order is not None else list(range(ncols // 128))
        for mi in order:
            for ti, (c0, n) in enumerate(tl):
                b = next_pbank()
                ps = bank(b, n)

                kper = (nkc + ksplit - 1) // ksplit
                for part in range(ksplit):
                    k0, k1 = part * kper, min(nkc, (part + 1) * kper)

                    def fn(eng, s=s, mi=mi, c0=c0, n=n, ps=ps, k0=k0, k1=k1):
                        ins = None
                        for kc in range(k0, k1):
                            ins = eng.matmul(ps, lhsT=wslot[s][:, kc, mi * 128:(mi + 1) * 128],
                                             rhs=in_ap(kc)[:, c0:c0 + n], start=(kc == 0), stop=(kc == nkc - 1))
                        return ins
                    P.op("pe", fn, reads=[("w", s)] + [in_res(kc) for kc in range(k0, k1)], writes=[("ps", b)])
                    if part < ksplit - 1:
                        yield
                consume(mi, ti, c0, n, ps, ("ps", b))
                yield
                if between and uidx in between:
                    for f_ in between[uidx]:
                        f_()
                        yield
                uidx += 1

    def act(out, in_, func, reads, writes, **kw):
        return P.op("act", lambda e: e.activation(out=out, in_=in_, func=func, **kw), reads=reads, writes=writes)

    def tt(out, in0, in1, op, reads, writes, eng="dve"):
        return P.op(eng, lambda e: e.tensor_tensor(out=out, in0=in0, in1=in1, op=op), reads=reads, writes=writes)

    def ts(out, in0, s1, op0, reads, writes, s2=None, op1=None, eng="dve"):
        if op1 is None:
            return P.op(eng, lambda e: e.tensor_scalar(out=out, in0=in0, scalar1=s1, scalar2=None, op0=op0),
                        reads=reads, writes=writes)
        return P.op(eng, lambda e: e.tensor_scalar(out=out, in0=in0, scalar1=s1, scalar2=s2, op0=op0, op1=op1),
                    reads=reads, writes=writes)

    def stt(out, in0, scalar, in1, op0, op1, reads, writes):
        return P.op("dve", lambda e: e.scalar_tensor_tensor(out=out, in0=in0, scalar=scalar, in1=in1, op0=op0, op1=op1),
                    reads=reads, writes=writes)

    sq_i = {"i": 0}

    def norm_stats(src_ap, src_res, nch, N, inv_n, rs, rs_res, parts=128):
        for (c0, n) in tiles_of(N):
            for c in range(nch):
                k = sq_i["i"] % 2
                sq_i["i"] += 1
                act(sqb[k][:, :n], src_ap(c)[:, c0:c0 + n], AF.Square, [src_res(c)], [("sqt", k)])
                P.op("pe", lambda e, k=k, n=n, c=c: e.matmul(bank(3, n), lhsT=onesB, rhs=sqb[k][:, :n],
                                                              start=(c == 0), stop=(c == nch - 1)),
                     reads=[("sqt", k), "cB"], writes=[("ps", 3)])
            act(rs[:, c0:c0 + n], bank(3, n), AF.Ln, [("ps", 3), "epsc"], [rs_res], scale=inv_n, bias=epsc[:, 0:1])
            act(rs[:, c0:c0 + n], rs[:, c0:c0 + n], AF.Exp, [rs_res], [rs_res], scale=-0.5)

    P.dma("sp", "c0", vecs.rearrange("p a b -> p (a b)"), vecs_d, writes=["vecs"])
    P.dma("sp", "c1", lbraw.rearrange("p a b -> p (a b)"), lbraw_d, writes=["lbraw"])
    P.dma("sp", "c2", gnorm, gnorm_d, writes=["gnorm"])
    P.dma("sp", "c3", esink, sinks_d, writes=["esink"])
    P.dma("sp", "c4", cF.rearrange("p a b -> p (a b)"), cF_d, writes=["cF"])
    P.dma("pool", "c5", cB, cB_d, writes=["cB"])
    act(esink, esink, AF.Exp, ["esink"], ["esink"])
    tt(lb, lbraw[:, 0, :], lbraw[:, 1, :], ALU.subtract, ["lbraw"], ["lb"])
    act(oml, lb, AF.Sigmoid, ["lb"], ["oml"], scale=-1.0)
    act(lb, lb, AF.Sigmoid, ["lb"], ["lb"])
    P.op("dve", lambda e: e.memset(S_all.rearrange("p a b -> p (a b)"), 0.0), writes=["S_all"])
    P.op("dve", lambda e: e.memset(epsc, EPS), writes=["epsc"])
    CRES = ["cF", "cB", "vecs", "lb", "oml", "gnorm", "esink"]

    VEC = {"npre0": 0, "npost0": 1, "fpre0": 2, "fpost0": 3, "npre1": 4, "npost1": 5, "fpre1": 6, "fpost1": 7, "kvn": 8}

    def gain(name, c):
        return vecs[:, VEC[name], c:c + 1]

    def hgrn_stage(ps_name, N, xn, has_out, nblk_prompt, nsamp, seq0, og):
        RY.reset()
        P.dma("sp", "scan", scanm[:, :N], scan_d[ps_name], writes=["scanm"])
        nblk = nblk_prompt + (1 if nsamp else 0)
        T = []
        for par in range(2):
            d = {}
            d["b1"] = RY.take(F32, [1, N])[:, 0, :]
            d["b2"] = RY.take(F32, [1, N])[:, 0, :]
            d["b3"] = RY.take(F32, [1, N])[:, 0, :]
            d["Kt"] = RY.take(BF16, [1, N])[:, 0, :]
            d["Vt"] = RY.take(BF16, [1, N])[:, 0, :]
            if has_out:
                d["Qt"] = RY.take(BF16, [1, N])[:, 0, :]
                d["gs"] = RY.take(BF16, [1, N])[:, 0, :]
            d["tok"] = RY.take(BF16, [nblk, 2, 128])
            T.append(d)
        S0 = [RY.take(F32, [8, 128]) for _ in range(2)] if nsamp else None
        cs = nblk_prompt * 128
        tl = tiles_of(N)

        def head_proj(hd):
            par = hd % 2
            t = T[par]
            R = lambda nm, par=par: ("T", par, nm)

            def consume(mi, ti, c0, n, ps, psres):
                sl = slice(c0, c0 + n)
                if mi == 2:
                    act(t["b1"][:, sl], ps, AF.Sigmoid, [psres], [R("b1")])
                    act(t["b3"][:, sl], ps, AF.Sigmoid, [psres], [R("b3")], scale=-1.0)
                elif mi == 0:
                    act(t["b2"][:, sl], ps, AF.Silu, [psres], [R("b2")])
                    tt(t["Qt"][:, sl], t["b2"][:, sl], t["b3"][:, sl], ALU.mult, [R("b2"), R("b3")], [R("Qt")])
                elif mi == 3:
                    act(t["Vt"][:, sl], ps, AF.Copy, [psres], [R("Vt")])
                else:
                    act(t["gs"][:, sl], ps, AF.Silu, [psres], [R("gs")])

            c_ln = lambda: act(t["b1"], t["b1"], AF.Ln, [R("b1"), "oml", "lb"], [R("b1")],
                               scale=oml[:, hd:hd + 1], bias=lb[:, hd:hd + 1])
            c_scan = lambda: P.op("dve", lambda e: e.tensor_tensor_scan(out=t["b1"], data0=scanm[:, :N], data1=t["b1"],
                                                                         initial=0.0, op0=ALU.mult, op1=ALU.add),
                                  reads=[R("b1"), "scanm"], writes=[R("b1")])
            c_enb = lambda: act(t["b2"], t["b1"], AF.Exp, [R("b1")], [R("b2")], scale=-1.0)
            c_kt = lambda: stt(t["Kt"], t["b3"], oml[:, hd:hd + 1], t["b2"], ALU.mult, ALU.mult,
                               [R("b3"), R("b2"), "oml"], [R("Kt")])
            c_eb = lambda: act(t["b3"], t["b1"], AF.Exp, [R("b1")], [R("b3")])
            nt = len(tl)
            if has_out:
                btw = {2 * nt - 1: [c_ln], 2 * nt: [c_scan], 3 * nt - 1: [c_enb, c_kt, c_eb]} if nt == 2 else \
                      {1: [c_ln, c_scan], 2: [c_enb, c_kt, c_eb]}
                yield from linear_g(w_in[hd], 16, 512, lambda kc: xn[:, kc, :], lambda kc: ("xn", kc), N, consume,
                                    mi_order=[2, 1, 3, 0], between=btw, ksplit=KSPLIT)
            else:
                def consume_p(mi, ti, c0, n, ps, psres):
                    consume(2 if mi == 0 else 3, ti, c0, n, ps, psres)
                btw = {nt - 1: [c_ln], nt: [c_scan], 2 * nt - 1: [c_enb, c_kt, c_eb]}
                yield from linear_g(w_in[hd][:, 256:512], 16, 256, lambda kc: xn[:, kc, :], lambda kc: ("xn", kc), N,
                                    consume_p, between=btw, ksplit=KSPLIT)

        def head_rec(hd):
            par = hd % 2
            t = T[par]
            R = lambda nm, par=par: ("T", par, nm)
            tok = t["tok"]
            for bi in range(nblk):
                c0 = bi * 128
                n = 128 if bi < nblk_prompt else nsamp
                hb = bi % 2
                trp = bankb(4)[:, hb * 512: hb * 512 + 256]

                def fn(eng, c0=c0, n=n, trp=trp):
                    eng.transpose(trp[0:n, 0:128], t["Kt"][:, c0:c0 + n], identB)
                    return eng.transpose(trp[0:n, 128:256], t["Vt"][:, c0:c0 + n], identB)
                P.op("pe", fn, reads=[R("Kt"), R("Vt"), "cB"], writes=[("ps", 4)])
                P.op("dve", lambda e, n=n, trp=trp, bi=bi: e.tensor_copy(
                    out=tok[0:n, bi, :, :].rearrange("p a b -> p (a b)"), in_=trp[0:n, :]),
                    reads=[("ps", 4)], writes=[R("tok")])
                yield
            P.op("dve", lambda e: e.tensor_copy(out=Sf[par][:, 0, :], in_=S_all[:, hd, :]),
                 reads=["S_all"], writes=[("Sf", par, 0)])
            if has_out:
                act(Sb[par][:, 0, :], S_all[:, hd, :], AF.Copy, ["S_all"], [("Sb", par, 0)])

            def emit_o(bi):
                c0 = bi * 128
                ab = bi % 2
                i0 = 2 * bi
                pso = bank(5, 128)

                def fn(eng):
                    eng.matmul(pso, lhsT=tok[:, bi, 1, :], rhs=ATm[ab], start=True, stop=False, skip_group_check=True)
                    eng.matmul(pso[:, 0:64], lhsT=Sb[par][:, i0 % 4, :], rhs=t["Qt"][:, c0:c0 + 64],
                               start=False, stop=True, skip_group_check=True)
                    return eng.matmul(pso[:, 64:128], lhsT=Sb[par][:, (i0 + 1) % 4, :], rhs=t["Qt"][:, c0 + 64:c0 + 128],
                                      start=False, stop=True, skip_group_check=True)
                P.op("pe", fn, reads=[R("tok"), ("ATm", ab), ("Sb", par, i0 % 4), ("Sb", par, (i0 + 1) % 4), R("Qt")],
                     writes=[("ps", 5)])
                act(t["b2"][:, c0:c0 + 128], pso, AF.Copy, [("ps", 5)], [R("b2")])

            for bi in range(nblk_prompt):
                c0 = bi * 128
                ab = bi % 2
                if has_out:
                    psA = PS[:, 3 * 512 + 256 + ab * 128: 3 * 512 + 256 + (ab + 1) * 128]
                    P.op("pe", lambda e, c0=c0, psA=psA: e.matmul(psA, lhsT=t["Kt"][:, c0:c0 + 128],
                                                                   rhs=t["Qt"][:, c0:c0 + 128], start=True, stop=True),
                         reads=[R("Kt"), R("Qt")], writes=[("ps", 3)])
                for j in range(2):
                    ub = 6 + j
                    r0 = 64 * j
                    P.op("pe", lambda e, ub=ub, r0=r0, bi=bi: e.matmul(bank(ub, 128), lhsT=tok[r0:r0 + 64, bi, 0, :],
                                                                       rhs=tok[r0:r0 + 64, bi, 1, :], start=True, stop=True),
                         reads=[R("tok")], writes=[("ps", ub)])
                yield
                if has_out and bi >= 1:
                    emit_o(bi - 1)
                if has_out:
                    tt(ATm[ab], psA, maskBD, ALU.mult, [("ps", 3), "cB"], [("ATm", ab)])
                for j in range(2):
                    i = 2 * bi + j
                    ub = 6 + j
                    e_ap = t["b3"][:, c0 + 64 * j + 63:c0 + 64 * j + 64]
                    act(Ue[par][:, j, :], bank(ub, 128), AF.Identity, [("ps", ub), R("b3")], [("Ue", par, j)], scale=e_ap)
                    stt(Sf[par][:, (i + 1) % 2, :], Sf[par][:, i % 2, :], e_ap, Ue[par][:, j, :], ALU.mult, ALU.add,
                        [("Sf", par, i % 2), ("Ue", par, j), R("b3")], [("Sf", par, (i + 1) % 2)])
                    if has_out:
                        P.op("dve", lambda e, i=i: e.tensor_copy(out=Sb[par][:, (i + 1) % 4, :], in_=Sf[par][:, (i + 1) % 2, :]),
                             reads=[("Sf", par, (i + 1) % 2)], writes=[("Sb", par, (i + 1) % 4)])
                yield
            if has_out:
                emit_o(nblk_prompt - 1)
            kfin = (2 * nblk_prompt) % 2
            P.op("dve", lambda e: e.tensor_copy(out=S_all[:, hd, :], in_=Sf[par][:, kfin, :]),
                 reads=[("Sf", par, kfin)], writes=["S_all"])
            yield

            if nsamp:
                bi = nblk_prompt
                s0 = S0[par]
                P.dma("sp", "s0in%d" % par, s0, st_in[seq0:seq0 + 8, hd].rearrange("s k v -> k s v"),
                      writes=[("S0", par)])
                act(S0b.rearrange("p a b -> p (a b)"), s0.rearrange("p a b -> p (a b)"), AF.Copy,
                    [("S0", par)], ["S0b"])
                psA = PS[0:32, 3 * 512 + 256: 3 * 512 + 256 + 32]
                P.op("pe", lambda e, psA=psA: e.matmul(psA, lhsT=t["Kt"][:, cs:cs + 32], rhs=t["Qt"][:, cs:cs + 32],
                                                       start=True, stop=True),
                     reads=[R("Kt"), R("Qt")], writes=[("ps", 3)])
                tt(ATm[0][0:32, 0:32], psA, maskS[0:32, :], ALU.mult, [("ps", 3), "cB"], [("ATm", 0)])
                pso = bank(5, 32)

                def fn(eng, pso=pso):
                    ins = eng.matmul(pso, lhsT=tok[0:32, bi, 1, :], rhs=ATm[0][0:32, 0:32], start=True, stop=False,
                                     skip_group_check=True)
                    for s in range(8):
                        ins = eng.matmul(pso[:, 4 * s:4 * s + 4], lhsT=S0b[:, s, :],
                                         rhs=t["Qt"][:, cs + 4 * s:cs + 4 * s + 4], start=False, stop=True,
                                         skip_group_check=True)
                    return ins
                P.op("pe", fn, reads=[R("tok"), ("ATm", 0), "S0b", R("Qt")], writes=[("ps", 5)])
                act(t["b2"][:, cs:cs + 32], pso, AF.Copy, [("ps", 5)], [R("b2")])
                tt(Vblk[0:32], tok[0:32, bi, 1, :].unsqueeze(1).to_broadcast([32, 8, 128]),
                   blockmask[0:32, :].unsqueeze(2).to_broadcast([32, 8, 128]), ALU.mult, [R("tok"), "cB"], ["Vblk"])
                psU2 = PS[:, 6 * 512: 8 * 512]

                def fn(eng):
                    eng.matmul(psU2[:, 0:512], lhsT=tok[0:32, bi, 0, :],
                               rhs=Vblk[0:32, 0:4, :].rearrange("p a b -> p (a b)"), start=True, stop=True)
                    return eng.matmul(psU2[:, 512:1024], lhsT=tok[0:32, bi, 0, :],
                                      rhs=Vblk[0:32, 4:8, :].rearrange("p a b -> p (a b)"), start=True, stop=True)
                P.op("pe", fn, reads=[R("tok"), "Vblk"], writes=[("ps", 6), ("ps", 7)])
                s0f = s0.rearrange("p a b -> p (a b)")
                tt(s0f, psU2, s0f, ALU.add, [("ps", 6), ("ps", 7), ("S0", par)], [("S0", par)])
                ebl = t["b3"][:, cs:cs + 32].rearrange("p (s t) -> p s t", t=4)[:, :, 3:4].to_broadcast([128, 8, 128])
                tt(s0, s0, ebl, ALU.mult, [("S0", par), R("b3")], [("S0", par)])
                P.dma("sp", "s0out%d" % par, st_out[seq0:seq0 + 8, hd].rearrange("s k v -> k s v"), s0,
                      reads=[("S0", par)])
                yield

            if has_out:
                rs = t["b1"]
                norm_stats(lambda c: t["b2"], lambda c: R("b2"), 1, N, 1.0 / 128, rs, R("b1"))
                stt(t["b2"], t["b2"], gnorm[:, hd:hd + 1], rs, ALU.mult, ALU.mult, [R("b2"), R("b1"), "gnorm"], [R("b2")])
                tt(og[:, hd, :], t["b2"], t["gs"], ALU.mult, [R("b2"), R("gs")], [("og", hd)])
                yield

        for _ in head_proj(0):
            pass
        for hd in range(16):
            gr = head_rec(hd)
            gp = head_proj(hd + 1) if hd < 15 else iter(())
            rdone = pdone = False
            while not (rdone and pdone):
                for _ in range(NREC_RND):
                    if not rdone:
                        try:
                            next(gr)
                        except StopIteration:
                            rdone = True
                for _ in range(NPROJ_RND):
                    if not pdone:
                        try:
                            next(gp)
                        except StopIteration:
                            pdone = True

    def load_prenorm(xsrc, N, stage, dst, gname, keep_h=None):
        for c in range(16):
            P.dma("sp", "x%d" % (c % 4), stage[:, c, :], xsrc[c * 128:(c + 1) * 128, :], writes=[("hT", c)])
        norm_stats(lambda c: stage[:, c, :], lambda c: ("hT", c), 16, N, 1.0 / D, rstd[0], "rstd0")
        for c in range(16):
            stt(dst[:, c, :], stage[:, c, :], gain(gname, c), rstd[0][:, :N], ALU.mult, ALU.mult,
                [("hT", c), "rstd0", "vecs"], [("xn", c)])

    def prenorm(hT, N, xn, gname):
        norm_stats(lambda c: hT[:, c, :], lambda c: ("hT", c), 16, N, 1.0 / D, rstd[0], "rstd0")
        for c in range(16):
            stt(xn[:, c, :], hT[:, c, :], gain(gname, c), rstd[0][:, :N], ALU.mult, ALU.mult,
                [("hT", c), "rstd0", "vecs"], [("xn", c)])

    def postnorm_add(hT, yT, N, gname):
        norm_stats(lambda c: yT[:, c, :], lambda c: ("yT", c), 16, N, 1.0 / D, rstd[1], "rstd0")
        for c in range(16):
            stt(yT[:, c, :], yT[:, c, :], gain(gname, c), rstd[1][:, :N], ALU.mult, ALU.mult,
                [("yT", c), "rstd0", "vecs"], [("yT", c)])
            tt(hT[:, c, :], hT[:, c, :], yT[:, c, :], ALU.add, [("hT", c), ("yT", c)], [("hT", c)])

    def proj_to_y(wsrc, in_ap, in_res, N, yT, tl=None, ycol0=0):
        for mb in range(4):
            def consume(mi, ti, c0, n, ps, psres, mb=mb):
                m = mb * 4 + mi
                act(yT[:, m, c0 - ycol0:c0 - ycol0 + n], ps, AF.Copy, [psres], [("yT", m)])
            linear(wsrc[:, mb * 512:(mb + 1) * 512], 16, 512, in_ap, in_res, N, consume, tl=tl)

    def ffn(layer, hT, xn, N, yT, hid, tl=None):
        groups = [(0, 16), (16, 16), (32, 12)]
        for gi, (j0, nj) in enumerate(groups):
            for jp in range(nj // 2):
                ja = j0 + 2 * jp

                def consume(mi, ti, c0, n, ps, psres, ja=ja, j0=j0):
                    j = ja + mi // 2
                    k = sq_i["i"] % 2
                    if mi % 2 == 0:
                        sq_i["i"] += 1
                        act(sqt[k][:, :n], ps, AF.Silu, [psres], [("sqt", k)])
                        consume.k = k
                    else:
                        k = consume.k
                        tt(hid[:, j - j0, c0:c0 + n], ps, sqt[k][:, :n], ALU.mult, [psres, ("sqt", k)], [("hid", j - j0)])
                s = load_w(w_gu[layer][:, ja * 256:(ja + 2) * 256], 16, 512)
                for jj in range(2):
                    for ti, (c0, n) in enumerate(tl or tiles_of(N)):
                        for half in range(2):
                            mi = jj * 2 + half
                            b = next_pbank()
                            ps = bank(b, n)

                            def fn(eng, s=s, mi=mi, c0=c0, n=n, ps=ps):
                                ins = None
                                for kc in range(16):
                                    ins = eng.matmul(ps, lhsT=wslot[s][:, kc, mi * 128:(mi + 1) * 128],
                                                     rhs=xn[:, kc, c0:c0 + n], start=(kc == 0), stop=(kc == 15))
                                return ins
                            P.op("pe", fn, reads=[("w", s)] + [("xn", kc) for kc in range(16)], writes=[("ps", b)])
                            consume(mi, ti, c0, n, ps, ("ps", b))
            for mb in range(4):
                def consume_d(mi, ti, c0, n, ps, psres, mb=mb, gi=gi):
                    m = mb * 4 + mi
                    if gi == 0:
                        act(yT[:, m, c0:c0 + n], ps, AF.Copy, [psres], [("yT", m)])
                    else:
                        tt(yT[:, m, c0:c0 + n], ps, yT[:, m, c0:c0 + n], ALU.add, [psres, ("yT", m)], [("yT", m)])
                linear(w_dn[layer][j0 * 128:(j0 + nj) * 128, mb * 512:(mb + 1) * 512], nj, 512,
                       lambda kc: hid[:, kc, :], lambda kc: ("hid", kc), N, consume_d, tl=tl)
        postnorm_add(hT, yT, N, "fpost%d" % layer)

    def run_pass(name, xsrc, N, nblk_prompt, own0, seq0):
        nq0 = own0
        cs = nblk_prompt * 128
        RH.reset(); RXU.reset()
        hT = RH.take(F32, [16, N])
        xn = RXU.take(BF16, [16, N])
        U = RXU.take(BF16, [16, N])
        load_prenorm(xsrc, N, hT, xn, "npre0")
        hgrn_stage(name, N, xn, True, nblk_prompt, 32, seq0, U)
        P.fence(pool=False)
        stage_done(name + '_hgrn')
        RY.reset()
        yT = RY.take(F32, [16, N])
        proj_to_y(w_out, lambda kc: U[:, kc, :], lambda kc: ("og", kc), N, yT)
        postnorm_add(hT, yT, N, "npost0")
        prenorm(hT, N, xn, "fpre0")
        stage_done(name + '_wout')
        ffn(0, hT, xn, N, yT, U)
        stage_done(name + '_ffn0')
        prenorm(hT, N, xn, "kvn")
        P.fence()
        RY.reset()
        NQ = N - nq0
        qrot = RY.take(BF16, [16, NQ])
        KT = RY.take(BF16, [4, 2, N])
        Vd = RY.take(BF16, [nblk_prompt + 1, 4, 128])
        P.op("dve", lambda e: e.memset(KT.rearrange("p a b c -> p (a b c)"), 0.0), writes=[("KT", g) for g in range(4)])
        for k2_ in range(2):
            P.op("dve", lambda e, k2_=k2_: e.memset(KcT[k2_].rearrange("p a b c -> p (a b c)"), 0.0),
                 writes=[("KcT", k2_)])
        qf = [RY.take(F32, [1, 336])[:, 0, :] for _ in range(2)]
        Vcc = [RY.take(BF16, [4, 128]) for _ in range(2)]
        for a_ in range(2):
            P.dma("sp", "rope", ropeT[:, a_, 0:N], rope_d[name][:, a_, :], writes=["rope"])
        tl = tiles_of(N)

        def rope_apply(dst, srcf, src_res, dst_res, c0, n, rc0, k):
            pp = bank(3, n)
            P.op("pe", lambda e: e.matmul(pp, lhsT=permF, rhs=srcf, start=True, stop=True),
                 reads=[src_res, "cF"], writes=[("ps", 3)])
            tt(nrm[k][:, :n], pp, ropeT[:, 1, rc0:rc0 + n], ALU.mult, [("ps", 3), "rope"], [("nrm", k)])
            tt(srcf, srcf, ropeT[:, 0, rc0:rc0 + n], ALU.mult, [src_res, "rope"], [src_res])
            if dst is not None:
                tt(dst, srcf, nrm[k][:, :n], ALU.add, [src_res, ("nrm", k)], [dst_res])

        kvcols = N - 160

        pend_kv = []
        kvcnt = {"i": 0}

        def consume_kv(mi, ti, c0, n, ps, psres, blk=0):
            k = kvcnt["i"] % 2
            kvcnt["i"] += 1
            g = mi
            act(qf[k][:, :n], ps, AF.Copy, [psres], [("qf", k)])

            def post_k():
                rope_apply(None, qf[k][:, :n], ("qf", k), None, c0, n, c0, k)
                tt(KT[0:64, g, 0, c0:c0 + n], qf[k][0:64, :n], nrm[k][0:64, :n], ALU.add,
                   [("qf", k), ("nrm", k)], [("KT", g)])
                tt(KT[64:128, g, 1, c0:c0 + n], qf[k][64:128, :n], nrm[k][64:128, :n], ALU.add,
                   [("qf", k), ("nrm", k)], [("KT", g)])
                lo = max(c0, kvcols); hi = c0 + n
                if hi > lo:
                    tt(kvst[0:64, 0, lo - kvcols:hi - kvcols], qf[k][0:64, lo - c0:hi - c0], nrm[k][0:64, lo - c0:hi - c0],
                       ALU.add, [("qf", k), ("nrm", k)], [("kvst", 0)])
                    if hi == N:
                        P.dma("sp", "kvo", kv_o[name][0, g], kvst[0:64, 0, :], reads=[("kvst", 0)])

            def post_v():
                P.op("dve", lambda e: e.tensor_copy(out=KTv[:, g, c0:c0 + n], in_=qf[k][:, :n]),
                     reads=[("qf", k)], writes=[("VT", g)])
                lo = max(c0, kvcols); hi = c0 + n
                if hi > lo:
                    P.op("dve", lambda e: e.tensor_copy(out=kvst[0:64, 1, lo - kvcols:hi - kvcols],
                                                        in_=qf[k][0:64, lo - c0:hi - c0]),
                         reads=[("qf", k)], writes=[("kvst", 1)])
                    if hi == N:
                        P.dma("sp", "kvo", kv_o[name][1, g], kvst[0:64, 1, :], reads=[("kvst", 1)])
            if pend_kv:
                pend_kv.pop(0)()
            pend_kv.append(post_k if blk == 0 else post_v)
        KTv = U[:, 0:4, :]
        linear(w_kv[:, 0:512], 16, 512, lambda kc: xn[:, kc, :], lambda kc: ("xn", kc), N,
               lambda *a: consume_kv(*a, blk=0))
        linear(w_kv[:, 512:1024], 16, 512, lambda kc: xn[:, kc, :], lambda kc: ("xn", kc), N,
               lambda *a: consume_kv(*a, blk=1))
        while pend_kv:
            pend_kv.pop(0)()
        for bi in range(nblk_prompt + 1):
            c0 = bi * 128
            n = 128 if bi < nblk_prompt else 32
            hb = bi % 2
            trp = bankb(4)[:, hb * 512:(hb + 1) * 512]

            def fn(eng, c0=c0, n=n, trp=trp):
                ins = None
                for g in range(4):
                    ins = eng.transpose(trp[0:n, g * 128:(g + 1) * 128], KTv[:, g, c0:c0 + n], identB)
                return ins
            P.op("pe", fn, reads=[("VT", g) for g in range(4)] + ["cB"], writes=[("ps", 4)])
            P.op("dve", lambda e, n=n, trp=trp, bi=bi: e.tensor_copy(
                out=Vd[0:n, bi, :, :].rearrange("p a b -> p (a b)"), in_=trp[0:n, :]),
                reads=[("ps", 4)], writes=[("Vd", bi)])

        stage_done(name + '_kv')
        prenorm(hT, N, xn, "npre1")
        qtl = [(nq0 + c0, n) for (c0, n) in tiles_of(NQ)]

        def consume_q(mb):
            def f(mi, ti, c0, n, ps, psres):
                m = mb * 4 + mi
                k = qcnt["i"] % 2
                qcnt["i"] += 1
                act(qf[k][:, :n], ps, AF.Copy, [psres], [("qf", k)])
                if pend_rope:
                    pend_rope.pop(0)()
                pend_rope.append(lambda m=m, k=k, c0=c0, n=n: rope_apply(
                    qrot[:, m, c0 - nq0:c0 - nq0 + n], qf[k][:, :n], ("qf", k), ("qrot", m), c0, n, c0, k))
            return f
        pend_rope = []
        qcnt = {"i": 0}
        for mb in range(4):
            linear(w_q[:, mb * 512:(mb + 1) * 512], 16, 512, lambda kc: xn[:, kc, :], lambda kc: ("xn", kc), N,
                   consume_q(mb), tl=qtl)
        while pend_rope:
            pend_rope.pop(0)()
        ao = U
        nqb = (cs - nq0) // 128
        unit = {"i": 0}
        pend_pv = []

        def prompt_unit(qb, g, hh):
            qc0 = qb * 128
            kcur = (nq0 + qb * 128) // 128
            u = unit["i"] % 2
            unit["i"] += 1
            scb = [u * 4 + 0, u * 4 + 1]
            ob, db = u * 4 + 2, u * 4 + 3
            if kcur - 1 >= 0:
                kprev = KT[:, g, :, (kcur - 1) * 128: kcur * 128]
                vprev = Vd[:, kcur - 1, g, :]
                prev_res = [("KT", g), ("Vd", kcur - 1)]
                mprev = MprevF if (name == "A" and qb == 0) else Mprev
            else:
                kprev = KTc[:, g, :, :]; vprev = Vc_carry[:, g, :]; prev_res = ["KTc", "Vcc"]
                mprev = Mprev
            kdiag = KT[:, g, :, kcur * 128:(kcur + 1) * 128]
            vdiag = Vd[:, kcur, g, :]
            for kb, (kap, mk) in enumerate([(kprev, mprev), (kdiag, Mdiag)]):
                def fn(eng, kb=kb, kap=kap, mk=mk):
                    ps = bank(scb[kb])
                    ins = eng.matmul(ps.rearrange("p (a b) -> p a b", a=4), lhsT=identB,
                                     rhs=mk.unsqueeze(1).to_broadcast([128, 4, 128]),
                                     start=True, stop=False, skip_group_check=True)
                    for h4 in range(4):
                        h = 8 * g + 4 * hh + h4
                        ins = eng.matmul(ps[:, h4 * 128:(h4 + 1) * 128], lhsT=kap[:, h % 2, :],
                                         rhs=qrot[:, h // 2, qc0:qc0 + 128], start=False, stop=True,
                                         skip_group_check=True)
                    return ins
                P.op("pe", fn, reads=prev_res + [("KT", g), "cB"] + [("qrot", (8 * g + 4 * hh) // 2 + i) for i in range(2)],
                     writes=[("ps", scb[kb])])
                act(PT[u][:, kb, :], bank(scb[kb]), AF.Exp, [("ps", scb[kb])], [("PT", u, kb)], scale=0.125)

            def pv():
                def fn(eng):
                    eng.matmul(bank(ob), lhsT=vprev, rhs=PT[u][:, 0, :], start=True, stop=False)
                    eng.matmul(bank(ob), lhsT=vdiag, rhs=PT[u][:, 1, :], start=False, stop=True)
                    eng.matmul(bank(db), lhsT=onesB, rhs=PT[u][:, 0, :], start=True, stop=False)
                    return eng.matmul(bank(db), lhsT=onesB, rhs=PT[u][:, 1, :], start=False, stop=True)
                P.op("pe", fn, reads=prev_res + [("Vd", kcur), ("PT", u, 0), ("PT", u, 1), "cB"],
                     writes=[("ps", ob), ("ps", db)])
                h0 = 8 * g + 4 * hh
                tt(nrm[u].rearrange("p (a b) -> p a b", a=4), bank(db).rearrange("p (a b) -> p a b", a=4),
                   esink[:, h0:h0 + 4].unsqueeze(2).to_broadcast([128, 4, 128]), ALU.add,
                   [("ps", db), "esink"], [("nrm", u)])
                act(nrm[u], nrm[u], AF.Ln, [("nrm", u)], [("nrm", u)])
                act(nrm[u], nrm[u], AF.Exp, [("nrm", u)], [("nrm", u)], scale=-1.0)
                c2 = h0 // 2
                for odd in range(2):
                    b0 = odd * 64
                    tt(ao[b0:b0 + 64, c2:c2 + 2, nq0 + qc0:nq0 + qc0 + 128],
                       bank(ob).rearrange("p (c two q) -> p c two q", c=2, two=2)[b0:b0 + 64, :, odd, :],
                       nrm[u].rearrange("p (c two q) -> p c two q", c=2, two=2)[b0:b0 + 64, :, odd, :],
                       ALU.mult, [("ps", ob), ("nrm", u)], [("ao", c2), ("ao", c2 + 1)])
            if pend_pv:
                pend_pv.pop(0)()
            pend_pv.append(pv)

        for qb in range(nqb):
            for g in range(4):
                for hh in range(2):
                    prompt_unit(qb, g, hh)
        while pend_pv:
            pend_pv.pop(0)()
        qs0 = cs - nq0
        pend_s = []
        for s in range(8):
            u = unit["i"] % 2
            unit["i"] += 1
            k2 = s % 2
            seq = seq0 + s
            scb = [u * 4 + 0, u * 4 + 1]
            ob, db = u * 4 + 2, u * 4 + 3
            sc_c = bank(scb[0], 128); sc_n = bank(scb[1], 128)
            P.dma("pool", "kc%d" % k2, KcT[k2][0:64, :, 0, :], ckT[seq, 0:64], writes=[("KcT", k2)])
            P.dma("pool", "kd%d" % k2, KcT[k2][64:128, :, 1, :], ckT[seq, 64:128], writes=[("KcT", k2)])
            P.dma("pool", "vc%d" % k2, Vcc[k2].rearrange("p a b -> p (a b)"),
                  cv[seq].rearrange("p a b -> p (a b)"), writes=[("Vcc", k2)])

            def fn(eng, s=s, k2=k2, sc_c=sc_c, sc_n=sc_n):
                eng.matmul(sc_c.rearrange("p (a b) -> p a b", a=4), lhsT=identB,
                           rhs=Mc.unsqueeze(1).to_broadcast([128, 4, 32]), start=True, stop=False, skip_group_check=True)
                ins = eng.matmul(sc_n[0:32, :].rearrange("p (a b) -> p a b", a=4), lhsT=identB[0:32, 0:32],
                                 rhs=Mnew[0:32, s * 32:(s + 1) * 32].unsqueeze(1).to_broadcast([32, 4, 32]),
                                 start=True, stop=False, skip_group_check=True)
                for g in range(4):
                    for h8 in range(8):
                        h = 8 * g + h8
                        col = g * 32 + h8 * 4
                        qq = qrot[:, h // 2, qs0 + 4 * s: qs0 + 4 * s + 4]
                        eng.matmul(sc_c[:, col:col + 4], lhsT=KcT[k2][:, g, h % 2, :], rhs=qq,
                                   start=False, stop=True, skip_group_check=True)
                        ins = eng.matmul(sc_n[0:32, col:col + 4], lhsT=KT[:, g, h % 2, cs:cs + 32], rhs=qq,
                                         start=False, stop=True, skip_group_check=True)
                return ins
            P.op("pe", fn, reads=[("KcT", k2), "cB"] + [("KT", g) for g in range(4)] + [("qrot", i) for i in range(16)],
                 writes=[("ps", scb[0]), ("ps", scb[1])])
            PTc = PT[u][:, 0, 0:128]; PTn = PT[u][0:32, 1, 0:128]
            act(PTc, sc_c, AF.Exp, [("ps", scb[0])], [("PT", u, 0)], scale=0.125)
            act(PTn, sc_n[0:32, :], AF.Exp, [("ps", scb[1])], [("PT", u, 1)], scale=0.125)

            def pv_s(s=s, u=u, k2=k2, ob=ob, db=db, PTc=PTc, PTn=PTn):
                def fn(eng, k2=k2, ob=ob, db=db, PTc=PTc, PTn=PTn):
                    for g in range(4):
                        o = bank(ob, 128)[:, g * 32:(g + 1) * 32]
                        eng.matmul(o, lhsT=Vcc[k2][:, g, :], rhs=PTc[:, g * 32:(g + 1) * 32], start=True, stop=False,
                                   skip_group_check=True)
                        eng.matmul(o, lhsT=Vd[0:32, nblk_prompt, g, :], rhs=PTn[:, g * 32:(g + 1) * 32],
                                   start=False, stop=True, skip_group_check=True)
                    eng.matmul(bank(db, 128), lhsT=onesB, rhs=PTc, start=True, stop=False)
                    return eng.matmul(bank(db, 128), lhsT=onesB[0:32, :], rhs=PTn, start=False, stop=True)
                P.op("pe", fn, reads=[("Vcc", k2), ("Vd", nblk_prompt), ("PT", u, 0), ("PT", u, 1), "cB"],
                     writes=[("ps", ob), ("ps", db)])
                nv = nrm[u][:, 0:128]
                tt(nv.rearrange("p (h t) -> p h t", t=4), bank(db, 128).rearrange("p (h t) -> p h t", t=4),
                   esink[:, 0:32].unsqueeze(2).to_broadcast([128, 32, 4]), ALU.add, [("ps", db), "esink"], [("nrm", u)])
                act(nv, nv, AF.Ln, [("nrm", u)], [("nrm", u)])
                act(nv, nv, AF.Exp, [("nrm", u)], [("nrm", u)], scale=-1.0)
                for odd in range(2):
                    b0 = odd * 64
                    src = bank(ob, 128).rearrange("p (c two t) -> p c two t", c=16, two=2)[b0:b0 + 64, :, odd, :]
                    nn = nv.rearrange("p (c two t) -> p c two t", c=16, two=2)[b0:b0 + 64, :, odd, :]
                    dst = ao[b0:b0 + 64, :, cs + 4 * s:cs + 4 * s + 4]
                    tt(dst, src, nn, ALU.mult, [("ps", ob), ("nrm", u)], [("ao", i) for i in range(16)])
            if pend_s:
                pend_s.pop(0)()
            pend_s.append(pv_s)
        while pend_s:
            pend_s.pop(0)()
        stage_done(name + '_attn')
        lastb = nblk_prompt - 1
        P.op("dve", lambda e: e.tensor_copy(out=KTc.rearrange("p a b c -> p (a b) c"),
                                            in_=KT.rearrange("p a b c -> p (a b) c")[:, :, lastb * 128:(lastb + 1) * 128]),
             reads=[("KT", g) for g in range(4)], writes=["KTc"])
        P.op("dve", lambda e: e.tensor_copy(out=Vc_carry, in_=Vd[:, lastb, :, :]), reads=[("Vd", lastb)], writes=["Vcc"])
        P.fence(pool=False)
        RY.reset()
        yT = RY.take(F32, [16, N])
        if nq0 > 0:
            P.op("dve", lambda e: e.memset(yT[:, :, 0:nq0], 0.0), writes=[("yT", c) for c in range(16)])
        proj_to_y(w_o, lambda kc: ao[:, kc, :], lambda kc: ("ao", kc), N, yT, tl=qtl)
        postnorm_add(hT, yT, N, "npost1")
        prenorm(hT, N, xn, "fpre1")
        ffn(1, hT, xn, N, yT, U, tl=qtl)
        for c in range(16):
            P.dma("sp", "yo%d" % (c % 4), yT_o[name][c * 128:(c + 1) * 128, :], hT[:, c, :], reads=[("hT", c)])
        P.fence(pool=False)

    RH.reset(); RXU.reset()
    xnP = RH.take(BF16, [16, NP_])
    stg = RXU.take(F32, [16, NP_ // 2])
    for hf in range(2):
        c0 = hf * (NP_ // 2)
        load_prenorm(xP[:, c0:c0 + NP_ // 2], NP_ // 2, stg, xnP[:, :, c0:c0 + NP_ // 2], "npre0")
    try:
        stage_done("setup")
        stage_done("P_norm")
        hgrn_stage("P", NP_, xnP, False, 7, 0, 0, None)
        P.fence(pool=False)
        stage_done("P")
        run_pass("A", xA, NA_, 5, 128, 0)
        stage_done("A")
        run_pass("B", xB, NB_, 4, 0, 8)
    except StopBuild:
        pass
    P.dma("sp", "spo", sp_out.rearrange("h k v -> k h v"), S_all, reads=["S_all"])
    P.emit()
    es.close()
    return nc


def _host_prep(inp):
    f32 = np.float32
    x_prompt = np.asarray(inp["x_prompt"], f32); x_sample = np.asarray(inp["x_sample"], f32)
    st = np.asarray(inp["state_hgrn"], f32)[0]
    ck = np.asarray(inp["cache_k_win"], f32); cvw = np.asarray(inp["cache_v_win"], f32)
    w_in = np.asarray(inp["hgrn_w_in"], f32)[0].reshape(D, 4, 16, 128)
    w_in_r = np.ascontiguousarray(w_in[:, [0, 3, 1, 2]].transpose(2, 0, 1, 3).reshape(16, D, 512))
    w_gu = np.asarray(inp["ffn_w_gate_up"], f32).reshape(2, D, 2, NJ, 128)
    w_gu_r = np.ascontiguousarray(w_gu.transpose(0, 1, 3, 2, 4).reshape(2, D, NJ * 256))
    wkv = np.asarray(inp["w_kv"], f32).reshape(D, 2, 4, 64)
    wkv_r = np.ascontiguousarray(np.concatenate([wkv, wkv], axis=3).reshape(D, 1024))

    def fm(v):
        return np.asarray(v, f32).reshape(16, 128).T
    vecs = np.stack([fm(inp["norm_mix_pre"][0]), fm(inp["norm_mix_post"][0]), fm(inp["norm_ffn_pre"][0]),
                     fm(inp["norm_ffn_post"][0]), fm(inp["norm_mix_pre"][1]), fm(inp["norm_mix_post"][1]),
                     fm(inp["norm_ffn_pre"][1]), fm(inp["norm_ffn_post"][1]), fm(inp["kv_norm"])], axis=1)
    vecs = np.ascontiguousarray(vecs.reshape(128, 144))
    lbr = np.asarray(inp["hgrn_lower_bounds"], f32)
    lbraw = np.ascontiguousarray(np.concatenate([fm(lbr[0]), fm(lbr[1])], axis=1))
    gnorm = np.ascontiguousarray(fm(inp["hgrn_g_norm"][0]))
    sinks = np.ascontiguousarray(np.broadcast_to(np.asarray(inp["attn_sinks"], f32)[0][None, :], (128, 32)))
    I = np.eye(128, dtype=f32)
    perm = np.zeros((128, 128), f32)
    for m in range(128):
        d = m % 64
        if d < 16:
            perm[m - d + ((d + 8) % 16), m] = 1.0
    cF = np.concatenate([I, np.ones((128, 128), f32), perm], axis=1)
    s_ = np.arange(128)[:, None]; t_ = np.arange(128)[None, :]
    maskBD = ((s_ // 64 == t_ // 64) & (s_ <= t_)).astype(f32)
    Mdiag = np.where(s_ <= t_, 0.0, NEG).astype(f32)
    Mprev = np.where(s_ > t_, 0.0, NEG).astype(f32)
    Mc = np.zeros((128, 32), f32)
    for h8 in range(8):
        for t in range(4):
            Mc[:, h8 * 4 + t] = np.where(np.arange(128) >= t + 1, 0.0, NEG)
    blockmask = np.zeros((128, 8), f32)
    for r in range(32):
        blockmask[r, r // 4] = 1.0
    maskS = np.zeros((128, 32), f32)
    for r in range(32):
        for c in range(32):
            maskS[r, c] = 1.0 if (r // 4 == c // 4 and r <= c) else 0.0
    Mnew = np.full((128, 256), NEG, f32)
    for r in range(32):
        for s in range(8):
            for h8 in range(8):
                for t in range(4):
                    if r // 4 == s and r % 4 <= t:
                        Mnew[r, s * 32 + h8 * 4 + t] = 0.0
    zpad = np.zeros((128, 128), f32)

    def scanmask(n_prompt, n_samp):
        m = np.ones(n_prompt + n_samp, f32)
        m[0:n_prompt:64] = 0.0
        m[n_prompt::4] = 0.0
        return np.ascontiguousarray(np.broadcast_to(m[None, :], (128, m.size)))

    inv = (np.float32(500000.0) ** (-np.arange(0, 16, 2, dtype=f32) / np.float32(16))).astype(f32)

    def rope_tab(pos):
        pos = np.asarray(pos, f32)
        ang = (pos[:, None] * inv[None, :]).astype(f32)
        cos = np.cos(ang).astype(f32); sin = np.sin(ang).astype(f32)
        C = np.ones((128, pos.size), f32); S = np.zeros((128, pos.size), f32)
        for p in range(128):
            d = p % 64
            if d < 16:
                C[p] = cos[:, d % 8]
                S[p] = -sin[:, d % 8] if d < 8 else sin[:, d % 8]
        return np.ascontiguousarray(np.stack([C, S], axis=1))

    shared = dict(w_in=w_in_r, w_out=np.ascontiguousarray(np.asarray(inp["hgrn_w_out"], f32)[0]), w_gu=w_gu_r,
                  w_dn=np.ascontiguousarray(np.asarray(inp["ffn_w_down"], f32)), w_kv=wkv_r,
                  w_q=np.ascontiguousarray(np.asarray(inp["attn_w_q"], f32)[0]),
                  w_o=np.ascontiguousarray(np.asarray(inp["attn_w_out"], f32)[0]),
                  vecs=vecs, lbraw=lbraw, gnorm=gnorm, sinks=sinks, cF=np.ascontiguousarray(cF),
                  scanP=scanmask(NP_, 0), scanA=scanmask(640, 32), scanB=scanmask(512, 32))
    maps = []
    for c in range(8):
        j, half = c // 2, c % 2
        t0 = half * 1024
        xs = x_prompt[j]
        own = xs[t0:t0 + 1024]
        if half == 1:
            halo = xs[t0 - 128:t0]; pref = xs[0:NP_]
            mpf = Mprev
        else:
            halo = np.zeros((128, D), f32); pref = np.zeros((NP_, D), f32)
            mpf = np.full((128, 128), NEG, f32)
        xs_s = x_sample[16 * c:16 * c + 16]
        sA = xs_s[0:8].reshape(32, D); sB = xs_s[8:16].reshape(32, D)
        xA = np.ascontiguousarray(np.concatenate([halo, own[0:512], sA], axis=0).T)
        xB = np.ascontiguousarray(np.concatenate([own[512:1024], sB], axis=0).T)
        cB = np.concatenate([I, np.ones((128, 128), f32), maskBD, Mdiag, Mprev, mpf, Mc, blockmask, maskS, Mnew, zpad],
                            axis=1)
        assert cB.shape[1] == 128 * 7 + 32 + 8 + 32 + 256, cB.shape
        posA = np.concatenate([np.arange(t0 - 128, t0 + 512), 8192 + np.tile(np.arange(4), 8)])
        posB = np.concatenate([np.arange(t0 + 512, t0 + 1024), 8192 + np.tile(np.arange(4), 8)])
        ckc = ck[16 * c:16 * c + 16]
        ckT = ckc.transpose(0, 3, 2, 1)
        ckT = np.ascontiguousarray(np.concatenate([ckT, ckT], axis=1))
        cvc = cvw[16 * c:16 * c + 16]
        cvd = np.ascontiguousarray(np.concatenate([cvc, cvc], axis=3))
        m = dict(shared)
        m.update(xP=np.ascontiguousarray(pref.T), xA=xA, xB=xB, st_in=np.ascontiguousarray(st[16 * c:16 * c + 16]),
                 ckT=ckT, cv=cvd, cB=np.ascontiguousarray(cB), ropeA=rope_tab(posA), ropeB=rope_tab(posB))
        maps.append(m)
    return maps


_NC = None


def kernel(**inputs):
    global _NC
    if _NC is None:
        _NC = build_program()
    maps = _host_prep(inputs)
    res = run_bass_kernel_spmd(_NC, maps, core_ids=list(range(8)))
    R = res.results
    f32 = np.float32
    y_prompt = np.zeros((4, 2048, D), f32); y_sample = np.zeros((128, 4, D), f32)
    S_p = np.zeros((1, 4, 16, 128, 128), f32); S_s = np.zeros((1, 128, 16, 128, 128), f32)
    k_p = np.zeros((4, 128, 4, 64), f32); v_p = np.zeros((4, 128, 4, 64), f32)
    k_s = np.zeros((128, 4, 4, 64), f32); v_s = np.zeros((128, 4, 4, 64), f32)
    for c in range(8):
        j, half = c // 2, c % 2
        t0 = half * 1024
        yA = R[c]["yT_A"].T; yB = R[c]["yT_B"].T
        y_prompt[j, t0:t0 + 512] = yA[128:640]
        y_prompt[j, t0 + 512:t0 + 1024] = yB[0:512]
        y_sample[16 * c:16 * c + 8] = yA[640:672].reshape(8, 4, D)
        y_sample[16 * c + 8:16 * c + 16] = yB[512:544].reshape(8, 4, D)
        S_s[0, 16 * c:16 * c + 16] = R[c]["st_out"]
        kvA = R[c]["kv_A"]; kvB = R[c]["kv_B"]
        k_s[16 * c:16 * c + 8] = kvA[0][:, :, 128:160].transpose(2, 0, 1).reshape(8, 4, 4, 64)
        v_s[16 * c:16 * c + 8] = kvA[1][:, :, 128:160].transpose(2, 0, 1).reshape(8, 4, 4, 64)
        k_s[16 * c + 8:16 * c + 16] = kvB[0][:, :, 128:160].transpose(2, 0, 1).reshape(8, 4, 4, 64)
        v_s[16 * c + 8:16 * c + 16] = kvB[1][:, :, 128:160].transpose(2, 0, 1).reshape(8, 4, 4, 64)
        if half == 1:
            S_p[0, j] = R[c]["sp_out"]
            k_p[j] = kvB[0][:, :, 0:128].transpose(2, 0, 1)
            v_p[j] = kvB[1][:, :, 0:128].transpose(2, 0, 1)
    return (y_prompt, y_sample, S_p, S_s, k_p, v_p, k_s, v_s)
```

```python
import numpy as np
from contextlib import ExitStack
import concourse.bass as bass
import concourse.mybir as mybir
from concourse.bass_utils import run_bass_kernel_spmd

F32 = mybir.dt.float32
BF16 = mybir.dt.bfloat16
U8 = mybir.dt.uint8
AF = mybir.ActivationFunctionType
ALU = mybir.AluOpType

D = 2048
NCH = 16
DFF = 5632
NJ = 44
EPS = 1e-6
NEG = -30000.0
NP_, NA_, NB_ = 896, 672, 544
ENG = ["pe", "act", "dve", "pool", "sp"]
KSPLIT = 2
NREC_RND, NPROJ_RND = 1, 1


def tiles_of(n):
    if n <= 512:
        return [(0, n)]
    h = n // 2
    return [(0, h), (h, n - h)]


class Prog:
    def __init__(self, nc, es):
        self.nc = nc
        self.es = es
        self.ops = {e: [] for e in ENG}
        self.cnt = {e: 0 for e in ENG}
        self.sem = {e: es.enter_context(nc.semaphore("sem_" + e)) for e in ENG}
        self.waited = {e: {} for e in ENG}
        self.lastw = {}
        self.readers = {}
        self.chan = {}

    def _deps(self, eng, reads, writes):
        deps = []
        for r in list(reads) + list(writes):
            t = self.lastw.get(r)
            if t is not None:
                deps.append(t)
        for r in writes:
            for t in self.readers.get(r, {}).values():
                deps.append(t)
        if eng == "pe":
            deps = [t for t in deps if t[0] != "pe"]
        return deps

    def _mkwaits(self, eng, deps):
        best = {}
        for k, v in deps:
            best[k] = max(best.get(k, 0), v)
        out = []
        for k, v in best.items():
            if self.waited[eng].get(k, 0) >= v:
                continue
            self.waited[eng][k] = v
            out.append((k, v))
        return out

    def _commit(self, tok, key, reads, writes):
        for r in writes:
            self.lastw[r] = tok
            self.readers[r] = {}
        for r in reads:
            if r in writes:
                continue
            self.readers.setdefault(r, {})[key] = tok

    def op(self, eng, fn, reads=(), writes=()):
        writes = list(writes) + [r for r in reads if isinstance(r, tuple) and r[0] == "ps" and r not in writes]
        deps = self._deps(eng, reads, writes)
        waits = self._mkwaits(eng, deps)
        self.cnt[eng] += 1
        tok = (eng, self.cnt[eng])
        self.ops[eng].append(("c", waits, fn))
        self._commit(tok, eng, reads, writes)
        return tok

    def dma(self, q, chan, out, in_, reads=(), writes=()):
        if chan not in self.chan:
            self.chan[chan] = [self.es.enter_context(self.nc.semaphore("ch_" + chan)), 0]
        ch = self.chan[chan]
        deps = self._deps(q, reads, writes)
        key = ("ch", chan)
        if ch[1] > 0:
            deps.append((key, ch[1]))
        waits = self._mkwaits(q, deps)
        ch[1] += 16
        tok = (key, ch[1])
        self.ops[q].append(("d", waits, (lambda e, o=out, i=in_: e.dma_start(out=o, in_=i)), ch[0]))
        self._commit(tok, key, reads, writes)
        return tok

    def fence(self, pool=True):
        allw = [(e, self.cnt[e]) for e in ENG if self.cnt[e] > 0]
        allw += [(("ch", c), v[1]) for c, v in self.chan.items() if v[1] > 0]
        for e in ENG:
            if e == "pool" and not pool:
                continue
            w = self._mkwaits(e, allw)
            if w:
                self.ops[e].append(("w", w))
        keep = lambda r: isinstance(r, tuple) and r[0] == "w"
        self.lastw = {r: t for r, t in self.lastw.items() if keep(r)}
        self.readers = {r: t for r, t in self.readers.items() if keep(r)}

    def _semof(self, k):
        if isinstance(k, tuple):
            return self.chan[k[1]][0]
        return self.sem[k]

    def emit(self):
        nc = self.nc
        self.fence()
        names = {"pe": "tensor", "act": "scalar", "dve": "vector", "pool": "gpsimd", "sp": "sync"}
        with nc.Block() as block:
            for e in ENG:
                def body(eng, e=e):
                    for item in self.ops[e]:
                        for k, v in item[1]:
                            eng.wait_ge(self._semof(k), v)
                        if item[0] == "c":
                            ins = item[2](eng)
                            ins.then_inc(self.sem[e], 1)
                        elif item[0] == "d":
                            item[2](eng).then_inc(item[3], 16)
                getattr(block, names[e])(body)


class Region:
    def __init__(self, nc, es, name, nbytes, parent=None, base=0):
        self.t = parent.t if parent is not None else es.enter_context(nc.sbuf_tensor(name, [128, nbytes], U8))
        self.n = nbytes
        self.off = 0
        self.base = base

    def sub(self, nbytes):
        assert self.off + nbytes <= self.n
        r = Region(None, None, None, nbytes, parent=self, base=self.base + self.off)
        self.off += nbytes
        return r

    def reset(self):
        self.off = 0

    def take(self, dtype, shape):
        esz = 4 if dtype == F32 else 2
        n = int(np.prod(shape)) * esz
        assert self.off + n <= self.n, (self.off, n, self.n)
        ap = self.t[:, self.base + self.off:self.base + self.off + n].bitcast(dtype)
        self.off += n
        if len(shape) == 2:
            return ap.rearrange("p (a b) -> p a b", a=shape[0])
        if len(shape) == 3:
            return ap.rearrange("p (a b c) -> p a b c", a=shape[0], b=shape[1])
        return ap


class StopBuild(Exception):
    pass


def build_program(stop=None):
    def stage_done(nm):
        if stop == nm:
            raise StopBuild()
    nc = bass.Bass("TRN2", target_bir_lowering=False)
    es = ExitStack()
    P = Prog(nc, es)

    def din(name, shape):
        return nc.dram_tensor(name, list(shape), F32, kind="ExternalInput").ap()

    def dout(name, shape):
        return nc.dram_tensor(name, list(shape), F32, kind="ExternalOutput").ap()

    xP = din("xP", [D, NP_]); xA = din("xA", [D, NA_]); xB = din("xB", [D, NB_])
    st_in = din("st_in", [16, 16, 128, 128])
    ckT = din("ckT", [16, 128, 4, 128])
    cv = din("cv", [16, 128, 4, 128])
    w_in = din("w_in", [16, D, 512])
    w_out = din("w_out", [D, D])
    w_gu = din("w_gu", [2, D, NJ * 256])
    w_dn = din("w_dn", [2, DFF, D])
    w_kv = din("w_kv", [D, 1024])
    w_q = din("w_q", [D, D]); w_o = din("w_o", [D, D])
    vecs_d = din("vecs", [128, 9 * 16])
    lbraw_d = din("lbraw", [128, 32])
    gnorm_d = din("gnorm", [128, 16])
    sinks_d = din("sinks", [128, 32])
    cF_d = din("cF", [128, 384])
    NCB = 128 * 7 + 32 + 8 + 32 + 256
    cB_d = din("cB", [128, NCB])
    scan_d = {"P": din("scanP", [128, NP_]), "A": din("scanA", [128, NA_]), "B": din("scanB", [128, NB_])}
    rope_d = {"A": din("ropeA", [128, 2, NA_]), "B": din("ropeB", [128, 2, NB_])}

    yT_o = {"A": dout("yT_A", [D, NA_]), "B": dout("yT_B", [D, NB_])}
    st_out = dout("st_out", [16, 16, 128, 128])
    sp_out = dout("sp_out", [16, 128, 128])
    kv_o = {"A": dout("kv_A", [2, 4, 64, 160]), "B": dout("kv_B", [2, 4, 64, 160])}

    RH = Region(nc, es, "RH", 43008)
    RXU = Region(nc, es, "RXU", 43008)
    RY = Region(nc, es, "RY", 43008)
    WS = Region(nc, es, "WS", 32768)
    RM = Region(nc, es, "RM", 51000)
    wslot = [WS.take(BF16, [16, 512]) for _ in range(2)]
    S_all = RM.take(F32, [16, 128])
    vecs = RM.take(F32, [9, 16])
    lbraw = RM.take(F32, [2, 16])
    lb = RM.take(F32, [1, 16])[:, 0, :]
    oml = RM.take(F32, [1, 16])[:, 0, :]
    gnorm = RM.take(F32, [1, 16])[:, 0, :]
    esink = RM.take(F32, [1, 32])[:, 0, :]
    cF = RM.take(F32, [3, 128])
    identF, onesF, permF = cF[:, 0, :], cF[:, 1, :], cF[:, 2, :]
    cB = RM.take(BF16, [1, NCB])[:, 0, :]
    identB = cB[:, 0:128]; onesB = cB[:, 128:256]; maskBD = cB[:, 256:384]
    Mdiag = cB[:, 384:512]; Mprev = cB[:, 512:640]; MprevF = cB[:, 640:768]
    Mc = cB[:, 768:800]; blockmask = cB[:, 800:808]; maskS = cB[:, 808:840]; Mnew = cB[:, 840:1096]
    zcol = cB[:, 1096:1224]
    scanm = RM.take(F32, [1, NP_])[:, 0, :]
    _r = RM.take(F32, [1, NA_])[:, 0, :]
    rstd = [_r, _r]
    sqt = [RM.take(F32, [1, 512])[:, 0, :] for _ in range(2)]
    sqb = [sqt[i].bitcast(BF16)[:, 0:512] for i in range(2)]
    AL = RM.sub(12288)
    S0b = AL.take(BF16, [8, 128])
    Vblk = AL.take(BF16, [8, 128])
    Ue = [AL.take(F32, [2, 128]) for _ in range(2)]
    Sf = [AL.take(F32, [2, 128]) for _ in range(2)]
    Sb = [AL.take(BF16, [4, 128]) for _ in range(2)]
    AL.reset()
    PT = [AL.take(BF16, [2, 512]) for _ in range(2)]
    KcT = [AL.take(BF16, [4, 2, 128]) for _ in range(2)]
    ATm = [RM.take(BF16, [1, 128])[:, 0, :] for _ in range(2)]
    KTc = RM.take(BF16, [4, 2, 128])
    Vc_carry = RM.take(BF16, [4, 128])
    epsc = RM.take(F32, [1, 8])[:, 0, :]
    kvst = RM.take(F32, [2, 160])
    nrm = [RM.take(F32, [1, 512])[:, 0, :] for _ in range(2)]
    ropeT = RM.take(F32, [2, NA_])

    PS = es.enter_context(nc.psum_tensor("PS", [128, 4096], F32))

    def bank(i, n=512):
        return PS[:, i * 512:i * 512 + n]

    def bankb(i):
        return PS[:, i * 512:(i + 1) * 512].bitcast(BF16)

    wstate = {"i": 0}

    def load_w(src_ap, nkc, ncols):
        s = wstate["i"] % 2
        wstate["i"] += 1
        P.dma("pool", "w%d" % s, wslot[s][:, 0:nkc, 0:ncols], src_ap.rearrange("(kc p) n -> p kc n", p=128),
              writes=[("w", s)])
        return s

    pstate = {"i": 0}

    def next_pbank():
        b = pstate["i"] % 3
        pstate["i"] += 1
        return b

    def linear(src2d, nkc, ncols, in_ap, in_res, N, consume, mi_order=None, tl=None):
        for _ in linear_g(src2d, nkc, ncols, in_ap, in_res, N, consume, mi_order=mi_order, tl=tl):
            pass

    def linear_g(src2d, nkc, ncols, in_ap, in_res, N, consume, mi_order=None, tl=None, between=None, ksplit=1):
        s = load_w(src2d, nkc, ncols)
        uidx = 0
        tl = tl or tiles_of(N)
        order = mi_order if mi_order is not None else list(range(ncols // 128))
        for mi in order:
            for ti, (c0, n) in enumerate(tl):
                b = next_pbank()
                ps = bank(b, n)

                kper = (nkc + ksplit - 1) // ksplit
                for part in range(ksplit):
                    k0, k1 = part * kper, min(nkc, (part + 1) * kper)

                    def fn(eng, s=s, mi=mi, c0=c0, n=n, ps=ps, k0=k0, k1=k1):
                        ins = None
                        for kc in range(k0, k1):
                            ins = eng.matmul(ps, lhsT=wslot[s][:, kc, mi * 128:(mi + 1) * 128],
                                             rhs=in_ap(kc)[:, c0:c0 + n], start=(kc == 0), stop=(kc == nkc - 1))
                        return ins
                    P.op("pe", fn, reads=[("w", s)] + [in_res(kc) for kc in range(k0, k1)], writes=[("ps", b)])
                    if part < ksplit - 1:
                        yield
                consume(mi, ti, c0, n, ps, ("ps", b))
                yield
                if between and uidx in between:
                    for f_ in between[uidx]:
                        f_()
                        yield
                uidx += 1

    def act(out, in_, func, reads, writes, **kw):
        return P.op("act", lambda e: e.activation(out=out, in_=in_, func=func, **kw), reads=reads, writes=writes)

    def tt(out, in0, in1, op, reads, writes, eng="dve"):
        return P.op(eng, lambda e: e.tensor_tensor(out=out, in0=in0, in1=in1, op=op), reads=reads, writes=writes)

    def ts(out, in0, s1, op0, reads, writes, s2=None, op1=None, eng="dve"):
        if op1 is None:
            return P.op(eng, lambda e: e.tensor_scalar(out=out, in0=in0, scalar1=s1, scalar2=None, op0=op0),
                        reads=reads, writes=writes)
        return P.op(eng, lambda e: e.tensor_scalar(out=out, in0=in0, scalar1=s1, scalar2=s2, op0=op0, op1=op1),
                    reads=reads, writes=writes)

    def stt(out, in0, scalar, in1, op0, op1, reads, writes):
        return P.op("dve", lambda e: e.scalar_tensor_tensor(out=out, in0=in0, scalar=scalar, in1=in1, op0=op0, op1=op1),
                    reads=reads, writes=writes)

    sq_i = {"i": 0}

    def norm_stats(src_ap, src_res, nch, N, inv_n, rs, rs_res, parts=128):
        for (c0, n) in tiles_of(N):
            for c in range(nch):
                k = sq_i["i"] % 2
                sq_i["i"] += 1
                act(sqb[k][:, :n], src_ap(c)[:, c0:c0 + n], AF.Square, [src_res(c)], [("sqt", k)])
                P.op("pe", lambda e, k=k, n=n, c=c: e.matmul(bank(3, n), lhsT=onesB, rhs=sqb[k][:, :n],
                                                              start=(c == 0), stop=(c == nch - 1)),
                     reads=[("sqt", k), "cB"], writes=[("ps", 3)])
            act(rs[:, c0:c0 + n], bank(3, n), AF.Ln, [("ps", 3), "epsc"], [rs_res], scale=inv_n, bias=epsc[:, 0:1])
            act(rs[:, c0:c0 + n], rs[:, c0:c0 + n], AF.Exp, [rs_res], [rs_res], scale=-0.5)

    P.dma("sp", "c0", vecs.rearrange("p a b -> p (a b)"), vecs_d, writes=["vecs"])
    P.dma("sp", "c1", lbraw.rearrange("p a b -> p (a b)"), lbraw_d, writes=["lbraw"])
    P.dma("sp", "c2", gnorm, gnorm_d, writes=["gnorm"])
    P.dma("sp", "c3", esink, sinks_d, writes=["esink"])
    P.dma("sp", "c4", cF.rearrange("p a b -> p (a b)"), cF_d, writes=["cF"])
    P.dma("pool", "c5", cB, cB_d, writes=["cB"])
    act(esink, esink, AF.Exp, ["esink"], ["esink"])
    tt(lb, lbraw[:, 0, :], lbraw[:, 1, :], ALU.subtract, ["lbraw"], ["lb"])
    act(oml, lb, AF.Sigmoid, ["lb"], ["oml"], scale=-1.0)
    act(lb, lb, AF.Sigmoid, ["lb"], ["lb"])
    P.op("dve", lambda e: e.memset(S_all.rearrange("p a b -> p (a b)"), 0.0), writes=["S_all"])
    P.op("dve", lambda e: e.memset(epsc, EPS), writes=["epsc"])
    CRES = ["cF", "cB", "vecs", "lb", "oml", "gnorm", "esink"]

    VEC = {"npre0": 0, "npost0": 1, "fpre0": 2, "fpost0": 3, "npre1": 4, "npost1": 5, "fpre1": 6, "fpost1": 7, "kvn": 8}

    def gain(name, c):
        return vecs[:, VEC[name], c:c + 1]

    def hgrn_stage(ps_name, N, xn, has_out, nblk_prompt, nsamp, seq0, og):
        RY.reset()
        P.dma("sp", "scan", scanm[:, :N], scan_d[ps_name], writes=["scanm"])
        nblk = nblk_prompt + (1 if nsamp else 0)
        T = []
        for par in range(2):
            d = {}
            d["b1"] = RY.take(F32, [1, N])[:, 0, :]
            d["b2"] = RY.take(F32, [1, N])[:, 0, :]
            d["b3"] = RY.take(F32, [1, N])[:, 0, :]
            d["Kt"] = RY.take(BF16, [1, N])[:, 0, :]
            d["Vt"] = RY.take(BF16, [1, N])[:, 0, :]
            if has_out:
                d["Qt"] = RY.take(BF16, [1, N])[:, 0, :]
                d["gs"] = RY.take(BF16, [1, N])[:, 0, :]
            d["tok"] = RY.take(BF16, [nblk, 2, 128])
            T.append(d)
        S0 = [RY.take(F32, [8, 128]) for _ in range(2)] if nsamp else None
        cs = nblk_prompt * 128
        tl = tiles_of(N)

        def head_proj(hd):
            par = hd % 2
            t = T[par]
            R = lambda nm, par=par: ("T", par, nm)

            def consume(mi, ti, c0, n, ps, psres):
                sl = slice(c0, c0 + n)
                if mi == 2:
                    act(t["b1"][:, sl], ps, AF.Sigmoid, [psres], [R("b1")])
                    act(t["b3"][:, sl], ps, AF.Sigmoid, [psres], [R("b3")], scale=-1.0)
                elif mi == 0:
                    act(t["b2"][:, sl], ps, AF.Silu, [psres], [R("b2")])
                    tt(t["Qt"][:, sl], t["b2"][:, sl], t["b3"][:, sl], ALU.mult, [R("b2"), R("b3")], [R("Qt")])
                elif mi == 3:
                    act(t["Vt"][:, sl], ps, AF.Copy, [psres], [R("Vt")])
                else:
                    act(t["gs"][:, sl], ps, AF.Silu, [psres], [R("gs")])

            c_ln = lambda: act(t["b1"], t["b1"], AF.Ln, [R("b1"), "oml", "lb"], [R("b1")],
                               scale=oml[:, hd:hd + 1], bias=lb[:, hd:hd + 1])
            c_scan = lambda: P.op("dve", lambda e: e.tensor_tensor_scan(out=t["b1"], data0=scanm[:, :N], data1=t["b1"],
                                                                         initial=0.0, op0=ALU.mult, op1=ALU.add),
                                  reads=[R("b1"), "scanm"], writes=[R("b1")])
            c_enb = lambda: act(t["b2"], t["b1"], AF.Exp, [R("b1")], [R("b2")], scale=-1.0)
            c_kt = lambda: stt(t["Kt"], t["b3"], oml[:, hd:hd + 1], t["b2"], ALU.mult, ALU.mult,
                               [R("b3"), R("b2"), "oml"], [R("Kt")])
            c_eb = lambda: act(t["b3"], t["b1"], AF.Exp, [R("b1")], [R("b3")])
            nt = len(tl)
            if has_out:
                btw = {2 * nt - 1: [c_ln], 2 * nt: [c_scan], 3 * nt - 1: [c_enb, c_kt, c_eb]} if nt == 2 else \
                      {1: [c_ln, c_scan], 2: [c_enb, c_kt, c_eb]}
                yield from linear_g(w_in[hd], 16, 512, lambda kc: xn[:, kc, :], lambda kc: ("xn", kc), N, consume,
                                    mi_order=[2, 1, 3, 0], between=btw, ksplit=KSPLIT)
            else:
                def consume_p(mi, ti, c0, n, ps, psres):
                    consume(2 if mi == 0 else 3, ti, c0, n, ps, psres)
                btw = {nt - 1: [c_ln], nt: [c_scan], 2 * nt - 1: [c_enb, c_kt, c_eb]}
                yield from linear_g(w_in[hd][:, 256:512], 16, 256, lambda kc: xn[:, kc, :], lambda kc: ("xn", kc), N,
                                    consume_p, between=btw, ksplit=KSPLIT)

        def head_rec(hd):
            par = hd % 2
            t = T[par]
            R = lambda nm, par=par: ("T", par, nm)
            tok = t["tok"]
            for bi in range(nblk):
                c0 = bi * 128
                n = 128 if bi < nblk_prompt else nsamp
                hb = bi % 2
                trp = bankb(4)[:, hb * 512: hb * 512 + 256]

                def fn(eng, c0=c0, n=n, trp=trp):
                    eng.transpose(trp[0:n, 0:128], t["Kt"][:, c0:c0 + n], identB)
                    return eng.transpose(trp[0:n, 128:256], t["Vt"][:, c0:c0 + n], identB)
                P.op("pe", fn, reads=[R("Kt"), R("Vt"), "cB"], writes=[("ps", 4)])
                P.op("dve", lambda e, n=n, trp=trp, bi=bi: e.tensor_copy(
                    out=tok[0:n, bi, :, :].rearrange("p a b -> p (a b)"), in_=trp[0:n, :]),
                    reads=[("ps", 4)], writes=[R("tok")])
                yield
            P.op("dve", lambda e: e.tensor_copy(out=Sf[par][:, 0, :], in_=S_all[:, hd, :]),
                 reads=["S_all"], writes=[("Sf", par, 0)])
            if has_out:
                act(Sb[par][:, 0, :], S_all[:, hd, :], AF.Copy, ["S_all"], [("Sb", par, 0)])

            def emit_o(bi):
                c0 = bi * 128
                ab = bi % 2
                i0 = 2 * bi
                pso = bank(5, 128)

                def fn(eng):
                    eng.matmul(pso, lhsT=tok[:, bi, 1, :], rhs=ATm[ab], start=True, stop=False, skip_group_check=True)
                    eng.matmul(pso[:, 0:64], lhsT=Sb[par][:, i0 % 4, :], rhs=t["Qt"][:, c0:c0 + 64],
                               start=False, stop=True, skip_group_check=True)
                    return eng.matmul(pso[:, 64:128], lhsT=Sb[par][:, (i0 + 1) % 4, :], rhs=t["Qt"][:, c0 + 64:c0 + 128],
                                      start=False, stop=True, skip_group_check=True)
                P.op("pe", fn, reads=[R("tok"), ("ATm", ab), ("Sb", par, i0 % 4), ("Sb", par, (i0 + 1) % 4), R("Qt")],
                     writes=[("ps", 5)])
                act(t["b2"][:, c0:c0 + 128], pso, AF.Copy, [("ps", 5)], [R("b2")])

            for bi in range(nblk_prompt):
                c0 = bi * 128
                ab = bi % 2
                if has_out:
                    psA = PS[:, 3 * 512 + 256 + ab * 128: 3 * 512 + 256 + (ab + 1) * 128]
                    P.op("pe", lambda e, c0=c0, psA=psA: e.matmul(psA, lhsT=t["Kt"][:, c0:c0 + 128],
                                                                   rhs=t["Qt"][:, c0:c0 + 128], start=True, stop=True),
                         reads=[R("Kt"), R("Qt")], writes=[("ps", 3)])
                for j in range(2):
                    ub = 6 + j
                    r0 = 64 * j
                    P.op("pe", lambda e, ub=ub, r0=r0, bi=bi: e.matmul(bank(ub, 128), lhsT=tok[r0:r0 + 64, bi, 0, :],
                                                                       rhs=tok[r0:r0 + 64, bi, 1, :], start=True, stop=True),
                         reads=[R("tok")], writes=[("ps", ub)])
                yield
                if has_out and bi >= 1:
                    emit_o(bi - 1)
                if has_out:
                    tt(ATm[ab], psA, maskBD, ALU.mult, [("ps", 3), "cB"], [("ATm", ab)])
                for j in range(2):
                    i = 2 * bi + j
                    ub = 6 + j
                    e_ap = t["b3"][:, c0 + 64 * j + 63:c0 + 64 * j + 64]
                    act(Ue[par][:, j, :], bank(ub, 128), AF.Identity, [("ps", ub), R("b3")], [("Ue", par, j)], scale=e_ap)
                    stt(Sf[par][:, (i + 1) % 2, :], Sf[par][:, i % 2, :], e_ap, Ue[par][:, j, :], ALU.mult, ALU.add,
                        [("Sf", par, i % 2), ("Ue", par, j), R("b3")], [("Sf", par, (i + 1) % 2)])
                    if has_out:
                        P.op("dve", lambda e, i=i: e.tensor_copy(out=Sb[par][:, (i + 1) % 4, :], in_=Sf[par][:, (i + 1) % 2, :]),
                             reads=[("Sf", par, (i + 1) % 2)], writes=[("Sb", par, (i + 1) % 4)])
                yield
            if has_out:
                emit_o(nblk_prompt - 1)
            kfin = (2 * nblk_prompt) % 2
            P.op("dve", lambda e: e.tensor_copy(out=S_all[:, hd, :], in_=Sf[par][:, kfin, :]),
                 reads=[("Sf", par, kfin)], writes=["S_all"])
            yield

            if nsamp:
                bi = nblk_prompt
                s0 = S0[par]
                P.dma("sp", "s0in%d" % par, s0, st_in[seq0:seq0 + 8, hd].rearrange("s k v -> k s v"),
                      writes=[("S0", par)])
                act(S0b.rearrange("p a b -> p (a b)"), s0.rearrange("p a b -> p (a b)"), AF.Copy,
                    [("S0", par)], ["S0b"])
                psA = PS[0:32, 3 * 512 + 256: 3 * 512 + 256 + 32]
                P.op("pe", lambda e, psA=psA: e.matmul(psA, lhsT=t["Kt"][:, cs:cs + 32], rhs=t["Qt"][:, cs:cs + 32],
                                                       start=True, stop=True),
                     reads=[R("Kt"), R("Qt")], writes=[("ps", 3)])
                tt(ATm[0][0:32, 0:32], psA, maskS[0:32, :], ALU.mult, [("ps", 3), "cB"], [("ATm", 0)])
                pso = bank(5, 32)

                def fn(eng, pso=pso):
                    ins = eng.matmul(pso, lhsT=tok[0:32, bi, 1, :], rhs=ATm[0][0:32, 0:32], start=True, stop=False,
                                     skip_group_check=True)
                    for s in range(8):
                        ins = eng.matmul(pso[:, 4 * s:4 * s + 4], lhsT=S0b[:, s, :],
                                         rhs=t["Qt"][:, cs + 4 * s:cs + 4 * s + 4], start=False, stop=True,
                                         skip_group_check=True)
                    return ins
                P.op("pe", fn, reads=[R("tok"), ("ATm", 0), "S0b", R("Qt")], writes=[("ps", 5)])
                act(t["b2"][:, cs:cs + 32], pso, AF.Copy, [("ps", 5)], [R("b2")])
                tt(Vblk[0:32], tok[0:32, bi, 1, :].unsqueeze(1).to_broadcast([32, 8, 128]),
                   blockmask[0:32, :].unsqueeze(2).to_broadcast([32, 8, 128]), ALU.mult, [R("tok"), "cB"], ["Vblk"])
                psU2 = PS[:, 6 * 512: 8 * 512]

                def fn(eng):
                    eng.matmul(psU2[:, 0:512], lhsT=tok[0:32, bi, 0, :],
                               rhs=Vblk[0:32, 0:4, :].rearrange("p a b -> p (a b)"), start=True, stop=True)
                    return eng.matmul(psU2[:, 512:1024], lhsT=tok[0:32, bi, 0, :],
                                      rhs=Vblk[0:32, 4:8, :].rearrange("p a b -> p (a b)"), start=True, stop=True)
                P.op("pe", fn, reads=[R("tok"), "Vblk"], writes=[("ps", 6), ("ps", 7)])
                s0f = s0.rearrange("p a b -> p (a b)")
                tt(s0f, psU2, s0f, ALU.add, [("ps", 6), ("ps", 7), ("S0", par)], [("S0", par)])
                ebl = t["b3"][:, cs:cs + 32].rearrange("p (s t) -> p s t", t=4)[:, :, 3:4].to_broadcast([128, 8, 128])
                tt(s0, s0, ebl, ALU.mult, [("S0", par), R("b3")], [("S0", par)])
                P.dma("sp", "s0out%d" % par, st_out[seq0:seq0 + 8, hd].rearrange("s k v -> k s v"), s0,
                      reads=[("S0", par)])
                yield

            if has_out:
                rs = t["b1"]
                norm_stats(lambda c: t["b2"], lambda c: R("b2"), 1, N, 1.0 / 128, rs, R("b1"))
                stt(t["b2"], t["b2"], gnorm[:, hd:hd + 1], rs, ALU.mult, ALU.mult, [R("b2"), R("b1"), "gnorm"], [R("b2")])
                tt(og[:, hd, :], t["b2"], t["gs"], ALU.mult, [R("b2"), R("gs")], [("og", hd)])
                yield

        for _ in head_proj(0):
            pass
        for hd in range(16):
            gr = head_rec(hd)
            gp = head_proj(hd + 1) if hd < 15 else iter(())
            rdone = pdone = False
            while not (rdone and pdone):
                for _ in range(NPROJ_RND):
                    if not pdone:
                        try:
                            next(gp)
                        except StopIteration:
                            pdone = True
                for _ in range(NREC_RND):
                    if not rdone:
                        try:
                            next(gr)
                        except StopIteration:
                            rdone = True

    def load_prenorm(xsrc, N, stage, dst, gname, keep_h=None):
        for c in range(16):
            P.dma("sp", "x%d" % (c % 4), stage[:, c, :], xsrc[c * 128:(c + 1) * 128, :], writes=[("hT", c)])
        norm_stats(lambda c: stage[:, c, :], lambda c: ("hT", c), 16, N, 1.0 / D, rstd[0], "rstd0")
        for c in range(16):
            stt(dst[:, c, :], stage[:, c, :], gain(gname, c), rstd[0][:, :N], ALU.mult, ALU.mult,
                [("hT", c), "rstd0", "vecs"], [("xn", c)])

    def prenorm(hT, N, xn, gname):
        norm_stats(lambda c: hT[:, c, :], lambda c: ("hT", c), 16, N, 1.0 / D, rstd[0], "rstd0")
        for c in range(16):
            stt(xn[:, c, :], hT[:, c, :], gain(gname, c), rstd[0][:, :N], ALU.mult, ALU.mult,
                [("hT", c), "rstd0", "vecs"], [("xn", c)])

    def postnorm_add(hT, yT, N, gname):
        norm_stats(lambda c: yT[:, c, :], lambda c: ("yT", c), 16, N, 1.0 / D, rstd[1], "rstd0")
        for c in range(16):
            stt(yT[:, c, :], yT[:, c, :], gain(gname, c), rstd[1][:, :N], ALU.mult, ALU.mult,
                [("yT", c), "rstd0", "vecs"], [("yT", c)])
            tt(hT[:, c, :], hT[:, c, :], yT[:, c, :], ALU.add, [("hT", c), ("yT", c)], [("hT", c)])

    def proj_to_y(wsrc, in_ap, in_res, N, yT, tl=None, ycol0=0):
        for mb in range(4):
            def consume(mi, ti, c0, n, ps, psres, mb=mb):
                m = mb * 4 + mi
                act(yT[:, m, c0 - ycol0:c0 - ycol0 + n], ps, AF.Copy, [psres], [("yT", m)])
            linear(wsrc[:, mb * 512:(mb + 1) * 512], 16, 512, in_ap, in_res, N, consume, tl=tl)

    def ffn(layer, hT, xn, N, yT, hid, tl=None):
        groups = [(0, 16), (16, 16), (32, 12)]
        for gi, (j0, nj) in enumerate(groups):
            for jp in range(nj // 2):
                ja = j0 + 2 * jp

                def consume(mi, ti, c0, n, ps, psres, ja=ja, j0=j0):
                    j = ja + mi // 2
                    k = sq_i["i"] % 2
                    if mi % 2 == 0:
                        sq_i["i"] += 1
                        act(sqt[k][:, :n], ps, AF.Silu, [psres], [("sqt", k)])
                        consume.k = k
                    else:
                        k = consume.k
                        tt(hid[:, j - j0, c0:c0 + n], ps, sqt[k][:, :n], ALU.mult, [psres, ("sqt", k)], [("hid", j - j0)])
                s = load_w(w_gu[layer][:, ja * 256:(ja + 2) * 256], 16, 512)
                for jj in range(2):
                    for ti, (c0, n) in enumerate(tl or tiles_of(N)):
                        for half in range(2):
                            mi = jj * 2 + half
                            b = next_pbank()
                            ps = bank(b, n)

                            def fn(eng, s=s, mi=mi, c0=c0, n=n, ps=ps):
                                ins = None
                                for kc in range(16):
                                    ins = eng.matmul(ps, lhsT=wslot[s][:, kc, mi * 128:(mi + 1) * 128],
                                                     rhs=xn[:, kc, c0:c0 + n], start=(kc == 0), stop=(kc == 15))
                                return ins
                            P.op("pe", fn, reads=[("w", s)] + [("xn", kc) for kc in range(16)], writes=[("ps", b)])
                            consume(mi, ti, c0, n, ps, ("ps", b))
            for mb in range(4):
                def consume_d(mi, ti, c0, n, ps, psres, mb=mb, gi=gi):
                    m = mb * 4 + mi
                    if gi == 0:
                        act(yT[:, m, c0:c0 + n], ps, AF.Copy, [psres], [("yT", m)])
                    else:
                        tt(yT[:, m, c0:c0 + n], ps, yT[:, m, c0:c0 + n], ALU.add, [psres, ("yT", m)], [("yT", m)])
                linear(w_dn[layer][j0 * 128:(j0 + nj) * 128, mb * 512:(mb + 1) * 512], nj, 512,
                       lambda kc: hid[:, kc, :], lambda kc: ("hid", kc), N, consume_d, tl=tl)
        postnorm_add(hT, yT, N, "fpost%d" % layer)

    def run_pass(name, xsrc, N, nblk_prompt, own0, seq0):
        nq0 = own0
        cs = nblk_prompt * 128
        RH.reset(); RXU.reset()
        hT = RH.take(F32, [16, N])
        xn = RXU.take(BF16, [16, N])
        U = RXU.take(BF16, [16, N])
        load_prenorm(xsrc, N, hT, xn, "npre0")
        hgrn_stage(name, N, xn, True, nblk_prompt, 32, seq0, U)
        P.fence(pool=False)
        stage_done(name + '_hgrn')
        RY.reset()
        yT = RY.take(F32, [16, N])
        proj_to_y(w_out, lambda kc: U[:, kc, :], lambda kc: ("og", kc), N, yT)
        postnorm_add(hT, yT, N, "npost0")
        prenorm(hT, N, xn, "fpre0")
        stage_done(name + '_wout')
        ffn(0, hT, xn, N, yT, U)
        stage_done(name + '_ffn0')
        prenorm(hT, N, xn, "kvn")
        P.fence()
        RY.reset()
        NQ = N - nq0
        qrot = RY.take(BF16, [16, NQ])
        KT = RY.take(BF16, [4, 2, N])
        Vd = RY.take(BF16, [nblk_prompt + 1, 4, 128])
        P.op("dve", lambda e: e.memset(KT.rearrange("p a b c -> p (a b c)"), 0.0), writes=[("KT", g) for g in range(4)])
        for k2_ in range(2):
            P.op("dve", lambda e, k2_=k2_: e.memset(KcT[k2_].rearrange("p a b c -> p (a b c)"), 0.0),
                 writes=[("KcT", k2_)])
        qf = [RY.take(F32, [1, 336])[:, 0, :] for _ in range(2)]
        Vcc = [RY.take(BF16, [4, 128]) for _ in range(2)]
        for a_ in range(2):
            P.dma("sp", "rope", ropeT[:, a_, 0:N], rope_d[name][:, a_, :], writes=["rope"])
        tl = tiles_of(N)

        def rope_apply(dst, srcf, src_res, dst_res, c0, n, rc0, k):
            pp = bank(3, n)
            P.op("pe", lambda e: e.matmul(pp, lhsT=permF, rhs=srcf, start=True, stop=True),
                 reads=[src_res, "cF"], writes=[("ps", 3)])
            tt(nrm[k][:, :n], pp, ropeT[:, 1, rc0:rc0 + n], ALU.mult, [("ps", 3), "rope"], [("nrm", k)])
            tt(srcf, srcf, ropeT[:, 0, rc0:rc0 + n], ALU.mult, [src_res, "rope"], [src_res])
            if dst is not None:
                tt(dst, srcf, nrm[k][:, :n], ALU.add, [src_res, ("nrm", k)], [dst_res])

        kvcols = N - 160

        def consume_kv(mi, ti, c0, n, ps, psres, blk=0):
            k = (mi + ti) % 2
            if blk == 0:
                g = mi
                act(qf[k][:, :n], ps, AF.Copy, [psres], [("qf", k)])
                rope_apply(None, qf[k][:, :n], ("qf", k), None, c0, n, c0, k)
                tt(KT[0:64, g, 0, c0:c0 + n], qf[k][0:64, :n], nrm[k][0:64, :n], ALU.add,
                   [("qf", k), ("nrm", k)], [("KT", g)])
                tt(KT[64:128, g, 1, c0:c0 + n], qf[k][64:128, :n], nrm[k][64:128, :n], ALU.add,
                   [("qf", k), ("nrm", k)], [("KT", g)])
                lo = max(c0, kvcols); hi = c0 + n
                if hi > lo:
                    tt(kvst[0:64, 0, lo - kvcols:hi - kvcols], qf[k][0:64, lo - c0:hi - c0], nrm[k][0:64, lo - c0:hi - c0],
                       ALU.add, [("qf", k), ("nrm", k)], [("kvst", 0)])
                    if hi == N:
                        P.dma("sp", "kvo", kv_o[name][0, g], kvst[0:64, 0, :], reads=[("kvst", 0)])
            else:
                g = mi
                act(qf[k][:, :n], ps, AF.Copy, [psres], [("qf", k)])
                P.op("dve", lambda e: e.tensor_copy(out=KTv[:, g, c0:c0 + n], in_=qf[k][:, :n]),
                     reads=[("qf", k)], writes=[("VT", g)])
                lo = max(c0, kvcols); hi = c0 + n
                if hi > lo:
                    P.op("dve", lambda e: e.tensor_copy(out=kvst[0:64, 1, lo - kvcols:hi - kvcols],
                                                        in_=qf[k][0:64, lo - c0:hi - c0]),
                         reads=[("qf", k)], writes=[("kvst", 1)])
                    if hi == N:
                        P.dma("sp", "kvo", kv_o[name][1, g], kvst[0:64, 1, :], reads=[("kvst", 1)])
        KTv = U[:, 0:4, :]
        linear(w_kv[:, 0:512], 16, 512, lambda kc: xn[:, kc, :], lambda kc: ("xn", kc), N,
               lambda *a: consume_kv(*a, blk=0))
        linear(w_kv[:, 512:1024], 16, 512, lambda kc: xn[:, kc, :], lambda kc: ("xn", kc), N,
               lambda *a: consume_kv(*a, blk=1))
        for bi in range(nblk_prompt + 1):
            c0 = bi * 128
            n = 128 if bi < nblk_prompt else 32
            hb = bi % 2
            trp = bankb(4)[:, hb * 512:(hb + 1) * 512]

            def fn(eng, c0=c0, n=n, trp=trp):
                ins = None
                for g in range(4):
                    ins = eng.transpose(trp[0:n, g * 128:(g + 1) * 128], KTv[:, g, c0:c0 + n], identB)
                return ins
            P.op("pe", fn, reads=[("VT", g) for g in range(4)] + ["cB"], writes=[("ps", 4)])
            P.op("dve", lambda e, n=n, trp=trp, bi=bi: e.tensor_copy(
                out=Vd[0:n, bi, :, :].rearrange("p a b -> p (a b)"), in_=trp[0:n, :]),
                reads=[("ps", 4)], writes=[("Vd", bi)])

        stage_done(name + '_kv')
        prenorm(hT, N, xn, "npre1")
        qtl = [(nq0 + c0, n) for (c0, n) in tiles_of(NQ)]

        def consume_q(mb):
            def f(mi, ti, c0, n, ps, psres):
                m = mb * 4 + mi
                k = qcnt["i"] % 2
                qcnt["i"] += 1
                act(qf[k][:, :n], ps, AF.Copy, [psres], [("qf", k)])
                if pend_rope:
                    pend_rope.pop(0)()
                pend_rope.append(lambda m=m, k=k, c0=c0, n=n: rope_apply(
                    qrot[:, m, c0 - nq0:c0 - nq0 + n], qf[k][:, :n], ("qf", k), ("qrot", m), c0, n, c0, k))
            return f
        pend_rope = []
        qcnt = {"i": 0}
        for mb in range(4):
            linear(w_q[:, mb * 512:(mb + 1) * 512], 16, 512, lambda kc: xn[:, kc, :], lambda kc: ("xn", kc), N,
                   consume_q(mb), tl=qtl)
        while pend_rope:
            pend_rope.pop(0)()
        ao = U
        nqb = (cs - nq0) // 128
        unit = {"i": 0}
        pend_pv = []

        def prompt_unit(qb, g, hh):
            qc0 = qb * 128
            kcur = (nq0 + qb * 128) // 128
            u = unit["i"] % 2
            unit["i"] += 1
            scb = [u * 4 + 0, u * 4 + 1]
            ob, db = u * 4 + 2, u * 4 + 3
            if kcur - 1 >= 0:
                kprev = KT[:, g, :, (kcur - 1) * 128: kcur * 128]
                vprev = Vd[:, kcur - 1, g, :]
                prev_res = [("KT", g), ("Vd", kcur - 1)]
                mprev = MprevF if (name == "A" and qb == 0) else Mprev
            else:
                kprev = KTc[:, g, :, :]; vprev = Vc_carry[:, g, :]; prev_res = ["KTc", "Vcc"]
                mprev = Mprev
            kdiag = KT[:, g, :, kcur * 128:(kcur + 1) * 128]
            vdiag = Vd[:, kcur, g, :]
            for kb, (kap, mk) in enumerate([(kprev, mprev), (kdiag, Mdiag)]):
                def fn(eng, kb=kb, kap=kap, mk=mk):
                    ps = bank(scb[kb])
                    ins = eng.matmul(ps.rearrange("p (a b) -> p a b", a=4), lhsT=identB,
                                     rhs=mk.unsqueeze(1).to_broadcast([128, 4, 128]),
                                     start=True, stop=False, skip_group_check=True)
                    for h4 in range(4):
                        h = 8 * g + 4 * hh + h4
                        ins = eng.matmul(ps[:, h4 * 128:(h4 + 1) * 128], lhsT=kap[:, h % 2, :],
                                         rhs=qrot[:, h // 2, qc0:qc0 + 128], start=False, stop=True,
                                         skip_group_check=True)
                    return ins
                P.op("pe", fn, reads=prev_res + [("KT", g), "cB"] + [("qrot", (8 * g + 4 * hh) // 2 + i) for i in range(2)],
                     writes=[("ps", scb[kb])])
                act(PT[u][:, kb, :], bank(scb[kb]), AF.Exp, [("ps", scb[kb])], [("PT", u, kb)], scale=0.125)

            def pv():
                def fn(eng):
                    eng.matmul(bank(ob), lhsT=vprev, rhs=PT[u][:, 0, :], start=True, stop=False)
                    eng.matmul(bank(ob), lhsT=vdiag, rhs=PT[u][:, 1, :], start=False, stop=True)
                    eng.matmul(bank(db), lhsT=onesB, rhs=PT[u][:, 0, :], start=True, stop=False)
                    return eng.matmul(bank(db), lhsT=onesB, rhs=PT[u][:, 1, :], start=False, stop=True)
                P.op("pe", fn, reads=prev_res + [("Vd", kcur), ("PT", u, 0), ("PT", u, 1), "cB"],
                     writes=[("ps", ob), ("ps", db)])
                h0 = 8 * g + 4 * hh
                tt(nrm[u].rearrange("p (a b) -> p a b", a=4), bank(db).rearrange("p (a b) -> p a b", a=4),
                   esink[:, h0:h0 + 4].unsqueeze(2).to_broadcast([128, 4, 128]), ALU.add,
                   [("ps", db), "esink"], [("nrm", u)])
                act(nrm[u], nrm[u], AF.Ln, [("nrm", u)], [("nrm", u)])
                act(nrm[u], nrm[u], AF.Exp, [("nrm", u)], [("nrm", u)], scale=-1.0)
                c2 = h0 // 2
                for odd in range(2):
                    b0 = odd * 64
                    tt(ao[b0:b0 + 64, c2:c2 + 2, nq0 + qc0:nq0 + qc0 + 128],
                       bank(ob).rearrange("p (c two q) -> p c two q", c=2, two=2)[b0:b0 + 64, :, odd, :],
                       nrm[u].rearrange("p (c two q) -> p c two q", c=2, two=2)[b0:b0 + 64, :, odd, :],
                       ALU.mult, [("ps", ob), ("nrm", u)], [("ao", c2), ("ao", c2 + 1)])
            if pend_pv:
                pend_pv.pop(0)()
            pend_pv.append(pv)

        for qb in range(nqb):
            for g in range(4):
                for hh in range(2):
                    prompt_unit(qb, g, hh)
        while pend_pv:
            pend_pv.pop(0)()
        qs0 = cs - nq0
        pend_s = []
        for s in range(8):
            u = unit["i"] % 2
            unit["i"] += 1
            k2 = s % 2
            seq = seq0 + s
            scb = [u * 4 + 0, u * 4 + 1]
            ob, db = u * 4 + 2, u * 4 + 3
            sc_c = bank(scb[0], 128); sc_n = bank(scb[1], 128)
            P.dma("pool", "kc%d" % k2, KcT[k2][0:64, :, 0, :], ckT[seq, 0:64], writes=[("KcT", k2)])
            P.dma("pool", "kd%d" % k2, KcT[k2][64:128, :, 1, :], ckT[seq, 64:128], writes=[("KcT", k2)])
            P.dma("pool", "vc%d" % k2, Vcc[k2].rearrange("p a b -> p (a b)"),
                  cv[seq].rearrange("p a b -> p (a b)"), writes=[("Vcc", k2)])

            def fn(eng, s=s, k2=k2, sc_c=sc_c, sc_n=sc_n):
                eng.matmul(sc_c.rearrange("p (a b) -> p a b", a=4), lhsT=identB,
                           rhs=Mc.unsqueeze(1).to_broadcast([128, 4, 32]), start=True, stop=False, skip_group_check=True)
                ins = eng.matmul(sc_n[0:32, :].rearrange("p (a b) -> p a b", a=4), lhsT=identB[0:32, 0:32],
                                 rhs=Mnew[0:32, s * 32:(s + 1) * 32].unsqueeze(1).to_broadcast([32, 4, 32]),
                                 start=True, stop=False, skip_group_check=True)
                for g in range(4):
                    for h8 in range(8):
                        h = 8 * g + h8
                        col = g * 32 + h8 * 4
                        qq = qrot[:, h // 2, qs0 + 4 * s: qs0 + 4 * s + 4]
                        eng.matmul(sc_c[:, col:col + 4], lhsT=KcT[k2][:, g, h % 2, :], rhs=qq,
                                   start=False, stop=True, skip_group_check=True)
                        ins = eng.matmul(sc_n[0:32, col:col + 4], lhsT=KT[:, g, h % 2, cs:cs + 32], rhs=qq,
                                         start=False, stop=True, skip_group_check=True)
                return ins
            P.op("pe", fn, reads=[("KcT", k2), "cB"] + [("KT", g) for g in range(4)] + [("qrot", i) for i in range(16)],
                 writes=[("ps", scb[0]), ("ps", scb[1])])
            PTc = PT[u][:, 0, 0:128]; PTn = PT[u][0:32, 1, 0:128]
            act(PTc, sc_c, AF.Exp, [("ps", scb[0])], [("PT", u, 0)], scale=0.125)
            act(PTn, sc_n[0:32, :], AF.Exp, [("ps", scb[1])], [("PT", u, 1)], scale=0.125)

            def pv_s(s=s, u=u, k2=k2, ob=ob, db=db, PTc=PTc, PTn=PTn):
                def fn(eng, k2=k2, ob=ob, db=db, PTc=PTc, PTn=PTn):
                    for g in range(4):
                        o = bank(ob, 128)[:, g * 32:(g + 1) * 32]
                        eng.matmul(o, lhsT=Vcc[k2][:, g, :], rhs=PTc[:, g * 32:(g + 1) * 32], start=True, stop=False,
                                   skip_group_check=True)
                        eng.matmul(o, lhsT=Vd[0:32, nblk_prompt, g, :], rhs=PTn[:, g * 32:(g + 1) * 32],
                                   start=False, stop=True, skip_group_check=True)
                    eng.matmul(bank(db, 128), lhsT=onesB, rhs=PTc, start=True, stop=False)
                    return eng.matmul(bank(db, 128), lhsT=onesB[0:32, :], rhs=PTn, start=False, stop=True)
                P.op("pe", fn, reads=[("Vcc", k2), ("Vd", nblk_prompt), ("PT", u, 0), ("PT", u, 1), "cB"],
                     writes=[("ps", ob), ("ps", db)])
                nv = nrm[u][:, 0:128]
                tt(nv.rearrange("p (h t) -> p h t", t=4), bank(db, 128).rearrange("p (h t) -> p h t", t=4),
                   esink[:, 0:32].unsqueeze(2).to_broadcast([128, 32, 4]), ALU.add, [("ps", db), "esink"], [("nrm", u)])
                act(nv, nv, AF.Ln, [("nrm", u)], [("nrm", u)])
                act(nv, nv, AF.Exp, [("nrm", u)], [("nrm", u)], scale=-1.0)
                for odd in range(2):
                    b0 = odd * 64
                    src = bank(ob, 128).rearrange("p (c two t) -> p c two t", c=16, two=2)[b0:b0 + 64, :, odd, :]
                    nn = nv.rearrange("p (c two t) -> p c two t", c=16, two=2)[b0:b0 + 64, :, odd, :]
                    dst = ao[b0:b0 + 64, :, cs + 4 * s:cs + 4 * s + 4]
                    tt(dst, src, nn, ALU.mult, [("ps", ob), ("nrm", u)], [("ao", i) for i in range(16)])
            if pend_s:
                pend_s.pop(0)()
            pend_s.append(pv_s)
        while pend_s:
            pend_s.pop(0)()
        stage_done(name + '_attn')
        lastb = nblk_prompt - 1
        P.op("dve", lambda e: e.tensor_copy(out=KTc.rearrange("p a b c -> p (a b) c"),
                                            in_=KT.rearrange("p a b c -> p (a b) c")[:, :, lastb * 128:(lastb + 1) * 128]),
             reads=[("KT", g) for g in range(4)], writes=["KTc"])
        P.op("dve", lambda e: e.tensor_copy(out=Vc_carry, in_=Vd[:, lastb, :, :]), reads=[("Vd", lastb)], writes=["Vcc"])
        P.fence(pool=False)
        RY.reset()
        yT = RY.take(F32, [16, N])
        if nq0 > 0:
            P.op("dve", lambda e: e.memset(yT[:, :, 0:nq0], 0.0), writes=[("yT", c) for c in range(16)])
        proj_to_y(w_o, lambda kc: ao[:, kc, :], lambda kc: ("ao", kc), N, yT, tl=qtl)
        postnorm_add(hT, yT, N, "npost1")
        prenorm(hT, N, xn, "fpre1")
        ffn(1, hT, xn, N, yT, U, tl=qtl)
        for c in range(16):
            P.dma("sp", "yo%d" % (c % 4), yT_o[name][c * 128:(c + 1) * 128, :], hT[:, c, :], reads=[("hT", c)])
        P.fence(pool=False)

    RH.reset(); RXU.reset()
    xnP = RH.take(BF16, [16, NP_])
    stg = RXU.take(F32, [16, NP_ // 2])
    for hf in range(2):
        c0 = hf * (NP_ // 2)
        load_prenorm(xP[:, c0:c0 + NP_ // 2], NP_ // 2, stg, xnP[:, :, c0:c0 + NP_ // 2], "npre0")
    try:
        stage_done("setup")
        stage_done("P_norm")
        hgrn_stage("P", NP_, xnP, False, 7, 0, 0, None)
        P.fence(pool=False)
        stage_done("P")
        run_pass("A", xA, NA_, 5, 128, 0)
        stage_done("A")
        run_pass("B", xB, NB_, 4, 0, 8)
    except StopBuild:
        pass
    P.dma("sp", "spo", sp_out.rearrange("h k v -> k h v"), S_all, reads=["S_all"])
    P.emit()
    es.close()
    return nc


def _host_prep(inp):
    f32 = np.float32
    x_prompt = np.asarray(inp["x_prompt"], f32); x_sample = np.asarray(inp["x_sample"], f32)
    st = np.asarray(inp["state_hgrn"], f32)[0]
    ck = np.asarray(inp["cache_k_win"], f32); cvw = np.asarray(inp["cache_v_win"], f32)
    w_in = np.asarray(inp["hgrn_w_in"], f32)[0].reshape(D, 4, 16, 128)
    w_in_r = np.ascontiguousarray(w_in[:, [0, 3, 1, 2]].transpose(2, 0, 1, 3).reshape(16, D, 512))
    w_gu = np.asarray(inp["ffn_w_gate_up"], f32).reshape(2, D, 2, NJ, 128)
    w_gu_r = np.ascontiguousarray(w_gu.transpose(0, 1, 3, 2, 4).reshape(2, D, NJ * 256))
    wkv = np.asarray(inp["w_kv"], f32).reshape(D, 2, 4, 64)
    wkv_r = np.ascontiguousarray(np.concatenate([wkv, wkv], axis=3).reshape(D, 1024))

    def fm(v):
        return np.asarray(v, f32).reshape(16, 128).T
    vecs = np.stack([fm(inp["norm_mix_pre"][0]), fm(inp["norm_mix_post"][0]), fm(inp["norm_ffn_pre"][0]),
                     fm(inp["norm_ffn_post"][0]), fm(inp["norm_mix_pre"][1]), fm(inp["norm_mix_post"][1]),
                     fm(inp["norm_ffn_pre"][1]), fm(inp["norm_ffn_post"][1]), fm(inp["kv_norm"])], axis=1)
    vecs = np.ascontiguousarray(vecs.reshape(128, 144))
    lbr = np.asarray(inp["hgrn_lower_bounds"], f32)
    lbraw = np.ascontiguousarray(np.concatenate([fm(lbr[0]), fm(lbr[1])], axis=1))
    gnorm = np.ascontiguousarray(fm(inp["hgrn_g_norm"][0]))
    sinks = np.ascontiguousarray(np.broadcast_to(np.asarray(inp["attn_sinks"], f32)[0][None, :], (128, 32)))
    I = np.eye(128, dtype=f32)
    perm = np.zeros((128, 128), f32)
    for m in range(128):
        d = m % 64
        if d < 16:
            perm[m - d + ((d + 8) % 16), m] = 1.0
    cF = np.concatenate([I, np.ones((128, 128), f32), perm], axis=1)
    s_ = np.arange(128)[:, None]; t_ = np.arange(128)[None, :]
    maskBD = ((s_ // 64 == t_ // 64) & (s_ <= t_)).astype(f32)
    Mdiag = np.where(s_ <= t_, 0.0, NEG).astype(f32)
    Mprev = np.where(s_ > t_, 0.0, NEG).astype(f32)
    Mc = np.zeros((128, 32), f32)
    for h8 in range(8):
        for t in range(4):
            Mc[:, h8 * 4 + t] = np.where(np.arange(128) >= t + 1, 0.0, NEG)
    blockmask = np.zeros((128, 8), f32)
    for r in range(32):
        blockmask[r, r // 4] = 1.0
    maskS = np.zeros((128, 32), f32)
    for r in range(32):
        for c in range(32):
            maskS[r, c] = 1.0 if (r // 4 == c // 4 and r <= c) else 0.0
    Mnew = np.full((128, 256), NEG, f32)
    for r in range(32):
        for s in range(8):
            for h8 in range(8):
                for t in range(4):
                    if r // 4 == s and r % 4 <= t:
                        Mnew[r, s * 32 + h8 * 4 + t] = 0.0
    zpad = np.zeros((128, 128), f32)

    def scanmask(n_prompt, n_samp):
        m = np.ones(n_prompt + n_samp, f32)
        m[0:n_prompt:64] = 0.0
        m[n_prompt::4] = 0.0
        return np.ascontiguousarray(np.broadcast_to(m[None, :], (128, m.size)))

    inv = (np.float32(500000.0) ** (-np.arange(0, 16, 2, dtype=f32) / np.float32(16))).astype(f32)

    def rope_tab(pos):
        pos = np.asarray(pos, f32)
        ang = (pos[:, None] * inv[None, :]).astype(f32)
        cos = np.cos(ang).astype(f32); sin = np.sin(ang).astype(f32)
        C = np.ones((128, pos.size), f32); S = np.zeros((128, pos.size), f32)
        for p in range(128):
            d = p % 64
            if d < 16:
                C[p] = cos[:, d % 8]
                S[p] = -sin[:, d % 8] if d < 8 else sin[:, d % 8]
        return np.ascontiguousarray(np.stack([C, S], axis=1))

    shared = dict(w_in=w_in_r, w_out=np.ascontiguousarray(np.asarray(inp["hgrn_w_out"], f32)[0]), w_gu=w_gu_r,
                  w_dn=np.ascontiguousarray(np.asarray(inp["ffn_w_down"], f32)), w_kv=wkv_r,
                  w_q=np.ascontiguousarray(np.asarray(inp["attn_w_q"], f32)[0]),
                  w_o=np.ascontiguousarray(np.asarray(inp["attn_w_out"], f32)[0]),
                  vecs=vecs, lbraw=lbraw, gnorm=gnorm, sinks=sinks, cF=np.ascontiguousarray(cF),
                  scanP=scanmask(NP_, 0), scanA=scanmask(640, 32), scanB=scanmask(512, 32))
    maps = []
    for c in range(8):
        j, half = c // 2, c % 2
        t0 = half * 1024
        xs = x_prompt[j]
        own = xs[t0:t0 + 1024]
        if half == 1:
            halo = xs[t0 - 128:t0]; pref = xs[0:NP_]
            mpf = Mprev
        else:
            halo = np.zeros((128, D), f32); pref = np.zeros((NP_, D), f32)
            mpf = np.full((128, 128), NEG, f32)
        xs_s = x_sample[16 * c:16 * c + 16]
        sA = xs_s[0:8].reshape(32, D); sB = xs_s[8:16].reshape(32, D)
        xA = np.ascontiguousarray(np.concatenate([halo, own[0:512], sA], axis=0).T)
        xB = np.ascontiguousarray(np.concatenate([own[512:1024], sB], axis=0).T)
        cB = np.concatenate([I, np.ones((128, 128), f32), maskBD, Mdiag, Mprev, mpf, Mc, blockmask, maskS, Mnew, zpad],
                            axis=1)
        assert cB.shape[1] == 128 * 7 + 32 + 8 + 32 + 256, cB.shape
        posA = np.concatenate([np.arange(t0 - 128, t0 + 512), 8192 + np.tile(np.arange(4), 8)])
        posB = np.concatenate([np.arange(t0 + 512, t0 + 1024), 8192 + np.tile(np.arange(4), 8)])
        ckc = ck[16 * c:16 * c + 16]
        ckT = ckc.transpose(0, 3, 2, 1)
        ckT = np.ascontiguousarray(np.concatenate([ckT, ckT], axis=1))
        cvc = cvw[16 * c:16 * c + 16]
        cvd = np.ascontiguousarray(np.concatenate([cvc, cvc], axis=3))
        m = dict(shared)
        m.update(xP=np.ascontiguousarray(pref.T), xA=xA, xB=xB, st_in=np.ascontiguousarray(st[16 * c:16 * c + 16]),
                 ckT=ckT, cv=cvd, cB=np.ascontiguousarray(cB), ropeA=rope_tab(posA), ropeB=rope_tab(posB))
        maps.append(m)
    return maps


_NC = None


def kernel(**inputs):
    global _NC
    if _NC is None:
        _NC = build_program()
    maps = _host_prep(inputs)
    res = run_bass_kernel_spmd(_NC, maps, core_ids=list(range(8)))
    R = res.results
    f32 = np.float32
    y_prompt = np.zeros((4, 2048, D), f32); y_sample = np.zeros((128, 4, D), f32)
    S_p = np.zeros((1, 4, 16, 128, 128), f32); S_s = np.zeros((1, 128, 16, 128, 128), f32)
    k_p = np.zeros((4, 128, 4, 64), f32); v_p = np.zeros((4, 128, 4, 64), f32)
    k_s = np.zeros((128, 4, 4, 64), f32); v_s = np.zeros((128, 4, 4, 64), f32)
    for c in range(8):
        j, half = c // 2, c % 2
        t0 = half * 1024
        yA = R[c]["yT_A"].T; yB = R[c]["yT_B"].T
        y_prompt[j, t0:t0 + 512] = yA[128:640]
        y_prompt[j, t0 + 512:t0 + 1024] = yB[0:512]
        y_sample[16 * c:16 * c + 8] = yA[640:672].reshape(8, 4, D)
        y_sample[16 * c + 8:16 * c + 16] = yB[512:544].reshape(8, 4, D)
        S_s[0, 16 * c:16 * c + 16] = R[c]["st_out"]
        kvA = R[c]["kv_A"]; kvB = R[c]["kv_B"]
        k_s[16 * c:16 * c + 8] = kvA[0][:, :, 128:160].transpose(2, 0, 1).reshape(8, 4, 4, 64)
        v_s[16 * c:16 * c + 8] = kvA[1][:, :, 128:160].transpose(2, 0, 1).reshape(8, 4, 4, 64)
        k_s[16 * c + 8:16 * c + 16] = kvB[0][:, :, 128:160].transpose(2, 0, 1).reshape(8, 4, 4, 64)
        v_s[16 * c + 8:16 * c + 16] = kvB[1][:, :, 128:160].transpose(2, 0, 1).reshape(8, 4, 4, 64)
        if half == 1:
            S_p[0, j] = R[c]["sp_out"]
            k_p[j] = kvB[0][:, :, 0:128].transpose(2, 0, 1)
            v_p[j] = kvB[1][:, :, 0:128].transpose(2, 0, 1)
    return (y_prompt, y_sample, S_p, S_s, k_p, v_p, k_s, v_s)
```

```python
import numpy as np
from contextlib import ExitStack
import concourse.bass as bass
import concourse.mybir as mybir
from concourse.bass_utils import run_bass_kernel_spmd

F32 = mybir.dt.float32
BF16 = mybir.dt.bfloat16
U8 = mybir.dt.uint8
AF = mybir.ActivationFunctionType
ALU = mybir.AluOpType

D = 2048
NCH = 16
DFF = 5632
NJ = 44
EPS = 1e-6
NEG = -30000.0
NP_, NA_, NB_ = 896, 672, 544
ENG = ["pe", "act", "dve", "pool", "sp"]
KSPLIT = 2
NREC_RND, NPROJ_RND = 1, 2


def tiles_of(n):
    if n <= 512:
        return [(0, n)]
    h = n // 2
    return [(0, h), (h, n - h)]


class Prog:
    def __init__(self, nc, es):
        self.nc = nc
        self.es = es
        self.ops = {e: [] for e in ENG}
        self.cnt = {e: 0 for e in ENG}
        self.sem = {e: es.enter_context(nc.semaphore("sem_" + e)) for e in ENG}
        self.waited = {e: {} for e in ENG}
        self.lastw = {}
        self.readers = {}
        self.chan = {}

    def _deps(self, eng, reads, writes):
        deps = []
        for r in list(reads) + list(writes):
            t = self.lastw.get(r)
            if t is not None:
                deps.append(t)
        for r in writes:
            for t in self.readers.get(r, {}).values():
                deps.append(t)
        if eng == "pe":
            deps = [t for t in deps if t[0] != "pe"]
        return deps

    def _mkwaits(self, eng, deps):
        best = {}
        for k, v in deps:
            best[k] = max(best.get(k, 0), v)
        out = []
        for k, v in best.items():
            if self.waited[eng].get(k, 0) >= v:
                continue
            self.waited[eng][k] = v
            out.append((k, v))
        return out

    def _commit(self, tok, key, reads, writes):
        for r in writes:
            self.lastw[r] = tok
            self.readers[r] = {}
        for r in reads:
            if r in writes:
                continue
            self.readers.setdefault(r, {})[key] = tok

    def op(self, eng, fn, reads=(), writes=()):
        writes = list(writes) + [r for r in reads if isinstance(r, tuple) and r[0] == "ps" and r not in writes]
        deps = self._deps(eng, reads, writes)
        waits = self._mkwaits(eng, deps)
        self.cnt[eng] += 1
        tok = (eng, self.cnt[eng])
        self.ops[eng].append(("c", waits, fn))
        self._commit(tok, eng, reads, writes)
        return tok

    def dma(self, q, chan, out, in_, reads=(), writes=()):
        if chan not in self.chan:
            self.chan[chan] = [self.es.enter_context(self.nc.semaphore("ch_" + chan)), 0]
        ch = self.chan[chan]
        deps = self._deps(q, reads, writes)
        key = ("ch", chan)
        if ch[1] > 0:
            deps.append((key, ch[1]))
        waits = self._mkwaits(q, deps)
        ch[1] += 16
        tok = (key, ch[1])
        self.ops[q].append(("d", waits, (lambda e, o=out, i=in_: e.dma_start(out=o, in_=i)), ch[0]))
        self._commit(tok, key, reads, writes)
        return tok

    def fence(self, pool=True):
        allw = [(e, self.cnt[e]) for e in ENG if self.cnt[e] > 0]
        allw += [(("ch", c), v[1]) for c, v in self.chan.items() if v[1] > 0]
        for e in ENG:
            if e == "pool" and not pool:
                continue
            w = self._mkwaits(e, allw)
            if w:
                self.ops[e].append(("w", w))
        keep = lambda r: isinstance(r, tuple) and r[0] == "w"
        self.lastw = {r: t for r, t in self.lastw.items() if keep(r)}
        self.readers = {r: t for r, t in self.readers.items() if keep(r)}

    def _semof(self, k):
        if isinstance(k, tuple):
            return self.chan[k[1]][0]
        return self.sem[k]

    def emit(self):
        nc = self.nc
        self.fence()
        names = {"pe": "tensor", "act": "scalar", "dve": "vector", "pool": "gpsimd", "sp": "sync"}
        with nc.Block() as block:
            for e in ENG:
                def body(eng, e=e):
                    for item in self.ops[e]:
                        for k, v in item[1]:
                            eng.wait_ge(self._semof(k), v)
                        if item[0] == "c":
                            ins = item[2](eng)
                            ins.then_inc(self.sem[e], 1)
                        elif item[0] == "d":
                            item[2](eng).then_inc(item[3], 16)
                getattr(block, names[e])(body)


class Region:
    def __init__(self, nc, es, name, nbytes, parent=None, base=0):
        self.t = parent.t if parent is not None else es.enter_context(nc.sbuf_tensor(name, [128, nbytes], U8))
        self.n = nbytes
        self.off = 0
        self.base = base

    def sub(self, nbytes):
        assert self.off + nbytes <= self.n
        r = Region(None, None, None, nbytes, parent=self, base=self.base + self.off)
        self.off += nbytes
        return r

    def reset(self):
        self.off = 0

    def take(self, dtype, shape):
        esz = 4 if dtype == F32 else 2
        n = int(np.prod(shape)) * esz
        assert self.off + n <= self.n, (self.off, n, self.n)
        ap = self.t[:, self.base + self.off:self.base + self.off + n].bitcast(dtype)
        self.off += n
        if len(shape) == 2:
            return ap.rearrange("p (a b) -> p a b", a=shape[0])
        if len(shape) == 3:
            return ap.rearrange("p (a b c) -> p a b c", a=shape[0], b=shape[1])
        return ap


class StopBuild(Exception):
    pass


def build_program(stop=None):
    def stage_done(nm):
        if stop == nm:
            raise StopBuild()
    nc = bass.Bass("TRN2", target_bir_lowering=False)
    es = ExitStack()
    P = Prog(nc, es)

    def din(name, shape):
        return nc.dram_tensor(name, list(shape), F32, kind="ExternalInput").ap()

    def dout(name, shape):
        return nc.dram_tensor(name, list(shape), F32, kind="ExternalOutput").ap()

    xP = din("xP", [D, NP_]); xA = din("xA", [D, NA_]); xB = din("xB", [D, NB_])
    st_in = din("st_in", [16, 16, 128, 128])
    ckT = din("ckT", [16, 128, 4, 128])
    cv = din("cv", [16, 128, 4, 128])
    w_in = din("w_in", [16, D, 512])
    w_out = din("w_out", [D, D])
    w_gu = din("w_gu", [2, D, NJ * 256])
    w_dn = din("w_dn", [2, DFF, D])
    w_kv = din("w_kv", [D, 1024])
    w_q = din("w_q", [D, D]); w_o = din("w_o", [D, D])
    vecs_d = din("vecs", [128, 9 * 16])
    lbraw_d = din("lbraw", [128, 32])
    gnorm_d = din("gnorm", [128, 16])
    sinks_d = din("sinks", [128, 32])
    cF_d = din("cF", [128, 384])
    NCB = 128 * 7 + 32 + 8 + 32 + 256
    cB_d = din("cB", [128, NCB])
    scan_d = {"P": din("scanP", [128, NP_]), "A": din("scanA", [128, NA_]), "B": din("scanB", [128, NB_])}
    rope_d = {"A": din("ropeA", [128, 2, NA_]), "B": din("ropeB", [128, 2, NB_])}

    yT_o = {"A": dout("yT_A", [D, NA_]), "B": dout("yT_B", [D, NB_])}
    st_out = dout("st_out", [16, 16, 128, 128])
    sp_out = dout("sp_out", [16, 128, 128])
    kv_o = {"A": dout("kv_A", [2, 4, 64, 160]), "B": dout("kv_B", [2, 4, 64, 160])}

    RH = Region(nc, es, "RH", 43008)
    RXU = Region(nc, es, "RXU", 43008)
    RY = Region(nc, es, "RY", 43008)
    WS = Region(nc, es, "WS", 32768)
    RM = Region(nc, es, "RM", 51000)
    wslot = [WS.take(BF16, [16, 512]) for _ in range(2)]
    S_all = RM.take(F32, [16, 128])
    vecs = RM.take(F32, [9, 16])
    lbraw = RM.take(F32, [2, 16])
    lb = RM.take(F32, [1, 16])[:, 0, :]
    oml = RM.take(F32, [1, 16])[:, 0, :]
    gnorm = RM.take(F32, [1, 16])[:, 0, :]
    esink = RM.take(F32, [1, 32])[:, 0, :]
    cF = RM.take(F32, [3, 128])
    identF, onesF, permF = cF[:, 0, :], cF[:, 1, :], cF[:, 2, :]
    cB = RM.take(BF16, [1, NCB])[:, 0, :]
    identB = cB[:, 0:128]; onesB = cB[:, 128:256]; maskBD = cB[:, 256:384]
    Mdiag = cB[:, 384:512]; Mprev = cB[:, 512:640]; MprevF = cB[:, 640:768]
    Mc = cB[:, 768:800]; blockmask = cB[:, 800:808]; maskS = cB[:, 808:840]; Mnew = cB[:, 840:1096]
    zcol = cB[:, 1096:1224]
    scanm = RM.take(F32, [1, NP_])[:, 0, :]
    _r = RM.take(F32, [1, NA_])[:, 0, :]
    rstd = [_r, _r]
    sqt = [RM.take(F32, [1, 512])[:, 0, :] for _ in range(2)]
    sqb = [sqt[i].bitcast(BF16)[:, 0:512] for i in range(2)]
    AL = RM.sub(12288)
    S0b = AL.take(BF16, [8, 128])
    Vblk = AL.take(BF16, [8, 128])
    Ue = [AL.take(F32, [2, 128]) for _ in range(2)]
    Sf = [AL.take(F32, [2, 128]) for _ in range(2)]
    Sb = [AL.take(BF16, [4, 128]) for _ in range(2)]
    AL.reset()
    PT = [AL.take(BF16, [2, 512]) for _ in range(2)]
    KcT = [AL.take(BF16, [4, 2, 128]) for _ in range(2)]
    ATm = [RM.take(BF16, [1, 128])[:, 0, :] for _ in range(2)]
    KTc = RM.take(BF16, [4, 2, 128])
    Vc_carry = RM.take(BF16, [4, 128])
    epsc = RM.take(F32, [1, 8])[:, 0, :]
    kvst = RM.take(F32, [2, 160])
    nrm = [RM.take(F32, [1, 512])[:, 0, :] for _ in range(2)]
    ropeT = RM.take(F32, [2, NA_])

    PS = es.enter_context(nc.psum_tensor("PS", [128, 4096], F32))

    def bank(i, n=512):
        return PS[:, i * 512:i * 512 + n]

    def bankb(i):
        return PS[:, i * 512:(i + 1) * 512].bitcast(BF16)

    wstate = {"i": 0}

    def load_w(src_ap, nkc, ncols):
        s = wstate["i"] % 2
        wstate["i"] += 1
        P.dma("pool", "w%d" % s, wslot[s][:, 0:nkc, 0:ncols], src_ap.rearrange("(kc p) n -> p kc n", p=128),
              writes=[("w", s)])
        return s

    pstate = {"i": 0}

    def next_pbank():
        b = pstate["i"] % 3
        pstate["i"] += 1
        return b

    def linear(src2d, nkc, ncols, in_ap, in_res, N, consume, mi_order=None, tl=None):
        for _ in linear_g(src2d, nkc, ncols, in_ap, in_res, N, consume, mi_order=mi_order, tl=tl):
            pass

    def linear_g(src2d, nkc, ncols, in_ap, in_res, N, consume, mi_order=None, tl=None, between=None, ksplit=1):
        s = load_w(src2d, nkc, ncols)
        uidx = 0
        tl = tl or tiles_of(N)
        order = mi_order if mi_order is not None else list(range(ncols // 128))
        for mi in order:
            for ti, (c0, n) in enumerate(tl):
                b = next_pbank()
                ps = bank(b, n)

                kper = (nkc + ksplit - 1) // ksplit
                for part in range(ksplit):
                    k0, k1 = part * kper, min(nkc, (part + 1) * kper)

                    def fn(eng, s=s, mi=mi, c0=c0, n=n, ps=ps, k0=k0, k1=k1):
                        ins = None
                        for kc in range(k0, k1):
                            ins = eng.matmul(ps, lhsT=wslot[s][:, kc, mi * 128:(mi + 1) * 128],
                                             rhs=in_ap(kc)[:, c0:c0 + n], start=(kc == 0), stop=(kc == nkc - 1))
                        return ins
                    P.op("pe", fn, reads=[("w", s)] + [in_res(kc) for kc in range(k0, k1)], writes=[("ps", b)])
                    if part < ksplit - 1:
                        yield
                consume(mi, ti, c0, n, ps, ("ps", b))
                yield
                if between and uidx in between:
                    for f_ in between[uidx]:
                        f_()
                        yield
                uidx += 1

    def act(out, in_, func, reads, writes, **kw):
        return P.op("act", lambda e: e.activation(out=out, in_=in_, func=func, **kw), reads=reads, writes=writes)

    def tt(out, in0, in1, op, reads, writes, eng="dve"):
        return P.op(eng, lambda e: e.tensor_tensor(out=out, in0=in0, in1=in1, op=op), reads=reads, writes=writes)

    def ts(out, in0, s1, op0, reads, writes, s2=None, op1=None, eng="dve"):
        if op1 is None:
            return P.op(eng, lambda e: e.tensor_scalar(out=out, in0=in0, scalar1=s1, scalar2=None, op0=op0),
                        reads=reads, writes=writes)
        return P.op(eng, lambda e: e.tensor_scalar(out=out, in0=in0, scalar1=s1, scalar2=s2, op0=op0, op1=op1),
                    reads=reads, writes=writes)

    def stt(out, in0, scalar, in1, op0, op1, reads, writes):
        return P.op("dve", lambda e: e.scalar_tensor_tensor(out=out, in0=in0, scalar=scalar, in1=in1, op0=op0, op1=op1),
                    reads=reads, writes=writes)

    sq_i = {"i": 0}

    def norm_stats(src_ap, src_res, nch, N, inv_n, rs, rs_res, parts=128):
        for (c0, n) in tiles_of(N):
            for c in range(nch):
                k = sq_i["i"] % 2
                sq_i["i"] += 1
                act(sqb[k][:, :n], src_ap(c)[:, c0:c0 + n], AF.Square, [src_res(c)], [("sqt", k)])
                P.op("pe", lambda e, k=k, n=n, c=c: e.matmul(bank(3, n), lhsT=onesB, rhs=sqb[k][:, :n],
                                                              start=(c == 0), stop=(c == nch - 1)),
                     reads=[("sqt", k), "cB"], writes=[("ps", 3)])
            act(rs[:, c0:c0 + n], bank(3, n), AF.Ln, [("ps", 3), "epsc"], [rs_res], scale=inv_n, bias=epsc[:, 0:1])
            act(rs[:, c0:c0 + n], rs[:, c0:c0 + n], AF.Exp, [rs_res], [rs_res], scale=-0.5)

    P.dma("sp", "c0", vecs.rearrange("p a b -> p (a b)"), vecs_d, writes=["vecs"])
    P.dma("sp", "c1", lbraw.rearrange("p a b -> p (a b)"), lbraw_d, writes=["lbraw"])
    P.dma("sp", "c2", gnorm, gnorm_d, writes=["gnorm"])
    P.dma("sp", "c3", esink, sinks_d, writes=["esink"])
    P.dma("sp", "c4", cF.rearrange("p a b -> p (a b)"), cF_d, writes=["cF"])
    P.dma("pool", "c5", cB, cB_d, writes=["cB"])
    act(esink, esink, AF.Exp, ["esink"], ["esink"])
    tt(lb, lbraw[:, 0, :], lbraw[:, 1, :], ALU.subtract, ["lbraw"], ["lb"])
    act(oml, lb, AF.Sigmoid, ["lb"], ["oml"], scale=-1.0)
    act(lb, lb, AF.Sigmoid, ["lb"], ["lb"])
    P.op("dve", lambda e: e.memset(S_all.rearrange("p a b -> p (a b)"), 0.0), writes=["S_all"])
    P.op("dve", lambda e: e.memset(epsc, EPS), writes=["epsc"])
    CRES = ["cF", "cB", "vecs", "lb", "oml", "gnorm", "esink"]

    VEC = {"npre0": 0, "npost0": 1, "fpre0": 2, "fpost0": 3, "npre1": 4, "npost1": 5, "fpre1": 6, "fpost1": 7, "kvn": 8}

    def gain(name, c):
        return vecs[:, VEC[name], c:c + 1]

    def hgrn_stage(ps_name, N, xn, has_out, nblk_prompt, nsamp, seq0, og):
        RY.reset()
        P.dma("sp", "scan", scanm[:, :N], scan_d[ps_name], writes=["scanm"])
        nblk = nblk_prompt + (1 if nsamp else 0)
        T = []
        for par in range(2):
            d = {}
            d["b1"] = RY.take(F32, [1, N])[:, 0, :]
            d["b2"] = RY.take(F32, [1, N])[:, 0, :]
            d["b3"] = RY.take(F32, [1, N])[:, 0, :]
            d["Kt"] = RY.take(BF16, [1, N])[:, 0, :]
            d["Vt"] = RY.take(BF16, [1, N])[:, 0, :]
            if has_out:
                d["Qt"] = RY.take(BF16, [1, N])[:, 0, :]
                d["gs"] = RY.take(BF16, [1, N])[:, 0, :]
            d["tok"] = RY.take(BF16, [nblk, 2, 128])
            T.append(d)
        S0 = [RY.take(F32, [8, 128]) for _ in range(2)] if nsamp else None
        cs = nblk_prompt * 128
        tl = tiles_of(N)

        def head_proj(hd):
            par = hd % 2
            t = T[par]
            R = lambda nm, par=par: ("T", par, nm)

            def consume(mi, ti, c0, n, ps, psres):
                sl = slice(c0, c0 + n)
                if mi == 2:
                    act(t["b1"][:, sl], ps, AF.Sigmoid, [psres], [R("b1")])
                    act(t["b3"][:, sl], ps, AF.Sigmoid, [psres], [R("b3")], scale=-1.0)
                elif mi == 0:
                    act(t["b2"][:, sl], ps, AF.Silu, [psres], [R("b2")])
                    tt(t["Qt"][:, sl], t["b2"][:, sl], t["b3"][:, sl], ALU.mult, [R("b2"), R("b3")], [R("Qt")])
                elif mi == 3:
                    act(t["Vt"][:, sl], ps, AF.Copy, [psres], [R("Vt")])
                else:
                    act(t["gs"][:, sl], ps, AF.Silu, [psres], [R("gs")])

            c_ln = lambda: act(t["b1"], t["b1"], AF.Ln, [R("b1"), "oml", "lb"], [R("b1")],
                               scale=oml[:, hd:hd + 1], bias=lb[:, hd:hd + 1])
            c_scan = lambda: P.op("dve", lambda e: e.tensor_tensor_scan(out=t["b1"], data0=scanm[:, :N], data1=t["b1"],
                                                                         initial=0.0, op0=ALU.mult, op1=ALU.add),
                                  reads=[R("b1"), "scanm"], writes=[R("b1")])
            c_enb = lambda: act(t["b2"], t["b1"], AF.Exp, [R("b1")], [R("b2")], scale=-1.0)
            c_kt = lambda: stt(t["Kt"], t["b3"], oml[:, hd:hd + 1], t["b2"], ALU.mult, ALU.mult,
                               [R("b3"), R("b2"), "oml"], [R("Kt")])
            c_eb = lambda: act(t["b3"], t["b1"], AF.Exp, [R("b1")], [R("b3")])
            nt = len(tl)
            if has_out:
                btw = {2 * nt - 1: [c_ln], 2 * nt: [c_scan], 3 * nt - 1: [c_enb, c_kt, c_eb]} if nt == 2 else \
                      {1: [c_ln, c_scan], 2: [c_enb, c_kt, c_eb]}
                yield from linear_g(w_in[hd], 16, 512, lambda kc: xn[:, kc, :], lambda kc: ("xn", kc), N, consume,
                                    mi_order=[2, 1, 3, 0], between=btw, ksplit=KSPLIT)
            else:
                def consume_p(mi, ti, c0, n, ps, psres):
                    consume(2 if mi == 0 else 3, ti, c0, n, ps, psres)
                btw = {nt - 1: [c_ln], nt: [c_scan], 2 * nt - 1: [c_enb, c_kt, c_eb]}
                yield from linear_g(w_in[hd][:, 256:512], 16, 256, lambda kc: xn[:, kc, :], lambda kc: ("xn", kc), N,
                                    consume_p, between=btw, ksplit=KSPLIT)

        def head_rec(hd):
            par = hd % 2
            t = T[par]
            R = lambda nm, par=par: ("T", par, nm)
            tok = t["tok"]
            for bi in range(nblk):
                c0 = bi * 128
                n = 128 if bi < nblk_prompt else nsamp
                hb = bi % 2
                trp = bankb(4)[:, hb * 512: hb * 512 + 256]

                def fn(eng, c0=c0, n=n, trp=trp):
                    eng.transpose(trp[0:n, 0:128], t["Kt"][:, c0:c0 + n], identB)
                    return eng.transpose(trp[0:n, 128:256], t["Vt"][:, c0:c0 + n], identB)
                P.op("pe", fn, reads=[R("Kt"), R("Vt"), "cB"], writes=[("ps", 4)])
                P.op("dve", lambda e, n=n, trp=trp, bi=bi: e.tensor_copy(
                    out=tok[0:n, bi, :, :].rearrange("p a b -> p (a b)"), in_=trp[0:n, :]),
                    reads=[("ps", 4)], writes=[R("tok")])
                yield
            P.op("dve", lambda e: e.tensor_copy(out=Sf[par][:, 0, :], in_=S_all[:, hd, :]),
                 reads=["S_all"], writes=[("Sf", par, 0)])
            if has_out:
                act(Sb[par][:, 0, :], S_all[:, hd, :], AF.Copy, ["S_all"], [("Sb", par, 0)])

            def emit_o(bi):
                c0 = bi * 128
                ab = bi % 2
                i0 = 2 * bi
                pso = bank(5, 128)

                def fn(eng):
                    eng.matmul(pso, lhsT=tok[:, bi, 1, :], rhs=ATm[ab], start=True, stop=False, skip_group_check=True)
                    eng.matmul(pso[:, 0:64], lhsT=Sb[par][:, i0 % 4, :], rhs=t["Qt"][:, c0:c0 + 64],
                               start=False, stop=True, skip_group_check=True)
                    return eng.matmul(pso[:, 64:128], lhsT=Sb[par][:, (i0 + 1) % 4, :], rhs=t["Qt"][:, c0 + 64:c0 + 128],
                                      start=False, stop=True, skip_group_check=True)
                P.op("pe", fn, reads=[R("tok"), ("ATm", ab), ("Sb", par, i0 % 4), ("Sb", par, (i0 + 1) % 4), R("Qt")],
                     writes=[("ps", 5)])
                act(t["b2"][:, c0:c0 + 128], pso, AF.Copy, [("ps", 5)], [R("b2")])

            for bi in range(nblk_prompt):
                c0 = bi * 128
                ab = bi % 2
                if has_out:
                    psA = PS[:, 3 * 512 + 256 + ab * 128: 3 * 512 + 256 + (ab + 1) * 128]
                    P.op("pe", lambda e, c0=c0, psA=psA: e.matmul(psA, lhsT=t["Kt"][:, c0:c0 + 128],
                                                                   rhs=t["Qt"][:, c0:c0 + 128], start=True, stop=True),
                         reads=[R("Kt"), R("Qt")], writes=[("ps", 3)])
                for j in range(2):
                    ub = 6 + j
                    r0 = 64 * j
                    P.op("pe", lambda e, ub=ub, r0=r0, bi=bi: e.matmul(bank(ub, 128), lhsT=tok[r0:r0 + 64, bi, 0, :],
                                                                       rhs=tok[r0:r0 + 64, bi, 1, :], start=True, stop=True),
                         reads=[R("tok")], writes=[("ps", ub)])
                yield
                if has_out and bi >= 1:
                    emit_o(bi - 1)
                if has_out:
                    tt(ATm[ab], psA, maskBD, ALU.mult, [("ps", 3), "cB"], [("ATm", ab)])
                for j in range(2):
                    i = 2 * bi + j
                    ub = 6 + j
                    e_ap = t["b3"][:, c0 + 64 * j + 63:c0 + 64 * j + 64]
                    act(Ue[par][:, j, :], bank(ub, 128), AF.Identity, [("ps", ub), R("b3")], [("Ue", par, j)], scale=e_ap)
                    stt(Sf[par][:, (i + 1) % 2, :], Sf[par][:, i % 2, :], e_ap, Ue[par][:, j, :], ALU.mult, ALU.add,
                        [("Sf", par, i % 2), ("Ue", par, j), R("b3")], [("Sf", par, (i + 1) % 2)])
                    if has_out:
                        P.op("dve", lambda e, i=i: e.tensor_copy(out=Sb[par][:, (i + 1) % 4, :], in_=Sf[par][:, (i + 1) % 2, :]),
                             reads=[("Sf", par, (i + 1) % 2)], writes=[("Sb", par, (i + 1) % 4)])
                yield
            if has_out:
                emit_o(nblk_prompt - 1)
            kfin = (2 * nblk_prompt) % 2
            P.op("dve", lambda e: e.tensor_copy(out=S_all[:, hd, :], in_=Sf[par][:, kfin, :]),
                 reads=[("Sf", par, kfin)], writes=["S_all"])
            yield

            if nsamp:
                bi = nblk_prompt
                s0 = S0[par]
                P.dma("sp", "s0in%d" % par, s0, st_in[seq0:seq0 + 8, hd].rearrange("s k v -> k s v"),
                      writes=[("S0", par)])
                act(S0b.rearrange("p a b -> p (a b)"), s0.rearrange("p a b -> p (a b)"), AF.Copy,
                    [("S0", par)], ["S0b"])
                psA = PS[0:32, 3 * 512 + 256: 3 * 512 + 256 + 32]
                P.op("pe", lambda e, psA=psA: e.matmul(psA, lhsT=t["Kt"][:, cs:cs + 32], rhs=t["Qt"][:, cs:cs + 32],
                                                       start=True, stop=True),
                     reads=[R("Kt"), R("Qt")], writes=[("ps", 3)])
                tt(ATm[0][0:32, 0:32], psA, maskS[0:32, :], ALU.mult, [("ps", 3), "cB"], [("ATm", 0)])
                pso = bank(5, 32)

                def fn(eng, pso=pso):
                    ins = eng.matmul(pso, lhsT=tok[0:32, bi, 1, :], rhs=ATm[0][0:32, 0:32], start=True, stop=False,
                                     skip_group_check=True)
                    for s in range(8):
                        ins = eng.matmul(pso[:, 4 * s:4 * s + 4], lhsT=S0b[:, s, :],
                                         rhs=t["Qt"][:, cs + 4 * s:cs + 4 * s + 4], start=False, stop=True,
                                         skip_group_check=True)
                    return ins
                P.op("pe", fn, reads=[R("tok"), ("ATm", 0), "S0b", R("Qt")], writes=[("ps", 5)])
                act(t["b2"][:, cs:cs + 32], pso, AF.Copy, [("ps", 5)], [R("b2")])
                tt(Vblk[0:32], tok[0:32, bi, 1, :].unsqueeze(1).to_broadcast([32, 8, 128]),
                   blockmask[0:32, :].unsqueeze(2).to_broadcast([32, 8, 128]), ALU.mult, [R("tok"), "cB"], ["Vblk"])
                psU2 = PS[:, 6 * 512: 8 * 512]

                def fn(eng):
                    eng.matmul(psU2[:, 0:512], lhsT=tok[0:32, bi, 0, :],
                               rhs=Vblk[0:32, 0:4, :].rearrange("p a b -> p (a b)"), start=True, stop=True)
                    return eng.matmul(psU2[:, 512:1024], lhsT=tok[0:32, bi, 0, :],
                                      rhs=Vblk[0:32, 4:8, :].rearrange("p a b -> p (a b)"), start=True, stop=True)
                P.op("pe", fn, reads=[R("tok"), "Vblk"], writes=[("ps", 6), ("ps", 7)])
                s0f = s0.rearrange("p a b -> p (a b)")
                tt(s0f, psU2, s0f, ALU.add, [("ps", 6), ("ps", 7), ("S0", par)], [("S0", par)])
                ebl = t["b3"][:, cs:cs + 32].rearrange("p (s t) -> p s t", t=4)[:, :, 3:4].to_broadcast([128, 8, 128])
                tt(s0, s0, ebl, ALU.mult, [("S0", par), R("b3")], [("S0", par)])
                P.dma("sp", "s0out%d" % par, st_out[seq0:seq0 + 8, hd].rearrange("s k v -> k s v"), s0,
                      reads=[("S0", par)])
                yield

            if has_out:
                rs = t["b1"]
                norm_stats(lambda c: t["b2"], lambda c: R("b2"), 1, N, 1.0 / 128, rs, R("b1"))
                stt(t["b2"], t["b2"], gnorm[:, hd:hd + 1], rs, ALU.mult, ALU.mult, [R("b2"), R("b1"), "gnorm"], [R("b2")])
                tt(og[:, hd, :], t["b2"], t["gs"], ALU.mult, [R("b2"), R("gs")], [("og", hd)])
                yield

        for _ in head_proj(0):
            pass
        for hd in range(16):
            gr = head_rec(hd)
            gp = head_proj(hd + 1) if hd < 15 else iter(())
            rdone = pdone = False
            while not (rdone and pdone):
                for _ in range(NREC_RND):
                    if not rdone:
                        try:
                            next(gr)
                        except StopIteration:
                            rdone = True
                for _ in range(NPROJ_RND):
                    if not pdone:
                        try:
                            next(gp)
                        except StopIteration:
                            pdone = True

    def load_prenorm(xsrc, N, stage, dst, gname, keep_h=None):
        for c in range(16):
            P.dma("sp", "x%d" % (c % 4), stage[:, c, :], xsrc[c * 128:(c + 1) * 128, :], writes=[("hT", c)])
        norm_stats(lambda c: stage[:, c, :], lambda c: ("hT", c), 16, N, 1.0 / D, rstd[0], "rstd0")
        for c in range(16):
            stt(dst[:, c, :], stage[:, c, :], gain(gname, c), rstd[0][:, :N], ALU.mult, ALU.mult,
                [("hT", c), "rstd0", "vecs"], [("xn", c)])

    def prenorm(hT, N, xn, gname):
        norm_stats(lambda c: hT[:, c, :], lambda c: ("hT", c), 16, N, 1.0 / D, rstd[0], "rstd0")
        for c in range(16):
            stt(xn[:, c, :], hT[:, c, :], gain(gname, c), rstd[0][:, :N], ALU.mult, ALU.mult,
                [("hT", c), "rstd0", "vecs"], [("xn", c)])

    def postnorm_add(hT, yT, N, gname):
        norm_stats(lambda c: yT[:, c, :], lambda c: ("yT", c), 16, N, 1.0 / D, rstd[1], "rstd0")
        for c in range(16):
            stt(yT[:, c, :], yT[:, c, :], gain(gname, c), rstd[1][:, :N], ALU.mult, ALU.mult,
                [("yT", c), "rstd0", "vecs"], [("yT", c)])
            tt(hT[:, c, :], hT[:, c, :], yT[:, c, :], ALU.add, [("hT", c), ("yT", c)], [("hT", c)])

    def proj_to_y(wsrc, in_ap, in_res, N, yT, tl=None, ycol0=0):
        for mb in range(4):
            def consume(mi, ti, c0, n, ps, psres, mb=mb):
                m = mb * 4 + mi
                act(yT[:, m, c0 - ycol0:c0 - ycol0 + n], ps, AF.Copy, [psres], [("yT", m)])
            linear(wsrc[:, mb * 512:(mb + 1) * 512], 16, 512, in_ap, in_res, N, consume, tl=tl)

    def ffn(layer, hT, xn, N, yT, hid, tl=None):
        groups = [(0, 16), (16, 16), (32, 12)]
        for gi, (j0, nj) in enumerate(groups):
            for jp in range(nj // 2):
                ja = j0 + 2 * jp

                def consume(mi, ti, c0, n, ps, psres, ja=ja, j0=j0):
                    j = ja + mi // 2
                    k = sq_i["i"] % 2
                    if mi % 2 == 0:
                        sq_i["i"] += 1
                        act(sqt[k][:, :n], ps, AF.Silu, [psres], [("sqt", k)])
                        consume.k = k
                    else:
                        k = consume.k
                        tt(hid[:, j - j0, c0:c0 + n], ps, sqt[k][:, :n], ALU.mult, [psres, ("sqt", k)], [("hid", j - j0)])
                s = load_w(w_gu[layer][:, ja * 256:(ja + 2) * 256], 16, 512)
                for jj in range(2):
                    for ti, (c0, n) in enumerate(tl or tiles_of(N)):
                        for half in range(2):
                            mi = jj * 2 + half
                            b = next_pbank()
                            ps = bank(b, n)

                            def fn(eng, s=s, mi=mi, c0=c0, n=n, ps=ps):
                                ins = None
                                for kc in range(16):
                                    ins = eng.matmul(ps, lhsT=wslot[s][:, kc, mi * 128:(mi + 1) * 128],
                                                     rhs=xn[:, kc, c0:c0 + n], start=(kc == 0), stop=(kc == 15))
                                return ins
                            P.op("pe", fn, reads=[("w", s)] + [("xn", kc) for kc in range(16)], writes=[("ps", b)])
                            consume(mi, ti, c0, n, ps, ("ps", b))
            for mb in range(4):
                def consume_d(mi, ti, c0, n, ps, psres, mb=mb, gi=gi):
                    m = mb * 4 + mi
                    if gi == 0:
                        act(yT[:, m, c0:c0 + n], ps, AF.Copy, [psres], [("yT", m)])
                    else:
                        tt(yT[:, m, c0:c0 + n], ps, yT[:, m, c0:c0 + n], ALU.add, [psres, ("yT", m)], [("yT", m)])
                linear(w_dn[layer][j0 * 128:(j0 + nj) * 128, mb * 512:(mb + 1) * 512], nj, 512,
                       lambda kc: hid[:, kc, :], lambda kc: ("hid", kc), N, consume_d, tl=tl)
        postnorm_add(hT, yT, N, "fpost%d" % layer)

    def run_pass(name, xsrc, N, nblk_prompt, own0, seq0):
        nq0 = own0
        cs = nblk_prompt * 128
        RH.reset(); RXU.reset()
        hT = RH.take(F32, [16, N])
        xn = RXU.take(BF16, [16, N])
        U = RXU.take(BF16, [16, N])
        load_prenorm(xsrc, N, hT, xn, "npre0")
        hgrn_stage(name, N, xn, True, nblk_prompt, 32, seq0, U)
        P.fence(pool=False)
        stage_done(name + '_hgrn')
        RY.reset()
        yT = RY.take(F32, [16, N])
        proj_to_y(w_out, lambda kc: U[:, kc, :], lambda kc: ("og", kc), N, yT)
        postnorm_add(hT, yT, N, "npost0")
        prenorm(hT, N, xn, "fpre0")
        stage_done(name + '_wout')
        ffn(0, hT, xn, N, yT, U)
        stage_done(name + '_ffn0')
        prenorm(hT, N, xn, "kvn")
        P.fence()
        RY.reset()
        NQ = N - nq0
        qrot = RY.take(BF16, [16, NQ])
        KT = RY.take(BF16, [4, 2, N])
        Vd = RY.take(BF16, [nblk_prompt + 1, 4, 128])
        P.op("dve", lambda e: e.memset(KT.rearrange("p a b c -> p (a b c)"), 0.0), writes=[("KT", g) for g in range(4)])
        for k2_ in range(2):
            P.op("dve", lambda e, k2_=k2_: e.memset(KcT[k2_].rearrange("p a b c -> p (a b c)"), 0.0),
                 writes=[("KcT", k2_)])
        qf = [RY.take(F32, [1, 336])[:, 0, :] for _ in range(2)]
        Vcc = [RY.take(BF16, [4, 128]) for _ in range(2)]
        for a_ in range(2):
            P.dma("sp", "rope", ropeT[:, a_, 0:N], rope_d[name][:, a_, :], writes=["rope"])
        tl = tiles_of(N)

        def rope_apply(dst, srcf, src_res, dst_res, c0, n, rc0, k):
            pp = bank(3, n)
            P.op("pe", lambda e: e.matmul(pp, lhsT=permF, rhs=srcf, start=True, stop=True),
                 reads=[src_res, "cF"], writes=[("ps", 3)])
            tt(nrm[k][:, :n], pp, ropeT[:, 1, rc0:rc0 + n], ALU.mult, [("ps", 3), "rope"], [("nrm", k)])
            tt(srcf, srcf, ropeT[:, 0, rc0:rc0 + n], ALU.mult, [src_res, "rope"], [src_res])
            if dst is not None:
                tt(dst, srcf, nrm[k][:, :n], ALU.add, [src_res, ("nrm", k)], [dst_res])

        kvcols = N - 160

        def consume_kv(mi, ti, c0, n, ps, psres, blk=0):
            k = (mi + ti) % 2
            if blk == 0:
                g = mi
                act(qf[k][:, :n], ps, AF.Copy, [psres], [("qf", k)])
                rope_apply(None, qf[k][:, :n], ("qf", k), None, c0, n, c0, k)
                tt(KT[0:64, g, 0, c0:c0 + n], qf[k][0:64, :n], nrm[k][0:64, :n], ALU.add,
                   [("qf", k), ("nrm", k)], [("KT", g)])
                tt(KT[64:128, g, 1, c0:c0 + n], qf[k][64:128, :n], nrm[k][64:128, :n], ALU.add,
                   [("qf", k), ("nrm", k)], [("KT", g)])
                lo = max(c0, kvcols); hi = c0 + n
                if hi > lo:
                    tt(kvst[0:64, 0, lo - kvcols:hi - kvcols], qf[k][0:64, lo - c0:hi - c0], nrm[k][0:64, lo - c0:hi - c0],
                       ALU.add, [("qf", k), ("nrm", k)], [("kvst", 0)])
                    if hi == N:
                        P.dma("sp", "kvo", kv_o[name][0, g], kvst[0:64, 0, :], reads=[("kvst", 0)])
            else:
                g = mi
                act(qf[k][:, :n], ps, AF.Copy, [psres], [("qf", k)])
                P.op("dve", lambda e: e.tensor_copy(out=KTv[:, g, c0:c0 + n], in_=qf[k][:, :n]),
                     reads=[("qf", k)], writes=[("VT", g)])
                lo = max(c0, kvcols); hi = c0 + n
                if hi > lo:
                    P.op("dve", lambda e: e.tensor_copy(out=kvst[0:64, 1, lo - kvcols:hi - kvcols],
                                                        in_=qf[k][0:64, lo - c0:hi - c0]),
                         reads=[("qf", k)], writes=[("kvst", 1)])
                    if hi == N:
                        P.dma("sp", "kvo", kv_o[name][1, g], kvst[0:64, 1, :], reads=[("kvst", 1)])
        KTv = U[:, 0:4, :]
        linear(w_kv[:, 0:512], 16, 512, lambda kc: xn[:, kc, :], lambda kc: ("xn", kc), N,
               lambda *a: consume_kv(*a, blk=0))
        linear(w_kv[:, 512:1024], 16, 512, lambda kc: xn[:, kc, :], lambda kc: ("xn", kc), N,
               lambda *a: consume_kv(*a, blk=1))
        for bi in range(nblk_prompt + 1):
            c0 = bi * 128
            n = 128 if bi < nblk_prompt else 32
            hb = bi % 2
            trp = bankb(4)[:, hb * 512:(hb + 1) * 512]

            def fn(eng, c0=c0, n=n, trp=trp):
                ins = None
                for g in range(4):
                    ins = eng.transpose(trp[0:n, g * 128:(g + 1) * 128], KTv[:, g, c0:c0 + n], identB)
                return ins
            P.op("pe", fn, reads=[("VT", g) for g in range(4)] + ["cB"], writes=[("ps", 4)])
            P.op("dve", lambda e, n=n, trp=trp, bi=bi: e.tensor_copy(
                out=Vd[0:n, bi, :, :].rearrange("p a b -> p (a b)"), in_=trp[0:n, :]),
                reads=[("ps", 4)], writes=[("Vd", bi)])

        stage_done(name + '_kv')
        prenorm(hT, N, xn, "npre1")
        qtl = [(nq0 + c0, n) for (c0, n) in tiles_of(NQ)]

        def consume_q(mb):
            def f(mi, ti, c0, n, ps, psres):
                m = mb * 4 + mi
                k = qcnt["i"] % 2
                qcnt["i"] += 1
                act(qf[k][:, :n], ps, AF.Copy, [psres], [("qf", k)])
                if pend_rope:
                    pend_rope.pop(0)()
                pend_rope.append(lambda m=m, k=k, c0=c0, n=n: rope_apply(
                    qrot[:, m, c0 - nq0:c0 - nq0 + n], qf[k][:, :n], ("qf", k), ("qrot", m), c0, n, c0, k))
            return f
        pend_rope = []
        qcnt = {"i": 0}
        for mb in range(4):
            linear(w_q[:, mb * 512:(mb + 1) * 512], 16, 512, lambda kc: xn[:, kc, :], lambda kc: ("xn", kc), N,
                   consume_q(mb), tl=qtl)
        while pend_rope:
            pend_rope.pop(0)()
        ao = U
        nqb = (cs - nq0) // 128
        unit = {"i": 0}
        pend_pv = []

        def prompt_unit(qb, g, hh):
            qc0 = qb * 128
            kcur = (nq0 + qb * 128) // 128
            u = unit["i"] % 2
            unit["i"] += 1
            scb = [u * 4 + 0, u * 4 + 1]
            ob, db = u * 4 + 2, u * 4 + 3
            if kcur - 1 >= 0:
                kprev = KT[:, g, :, (kcur - 1) * 128: kcur * 128]
                vprev = Vd[:, kcur - 1, g, :]
                prev_res = [("KT", g), ("Vd", kcur - 1)]
                mprev = MprevF if (name == "A" and qb == 0) else Mprev
            else:
                kprev = KTc[:, g, :, :]; vprev = Vc_carry[:, g, :]; prev_res = ["KTc", "Vcc"]
                mprev = Mprev
            kdiag = KT[:, g, :, kcur * 128:(kcur + 1) * 128]
            vdiag = Vd[:, kcur, g, :]
            for kb, (kap, mk) in enumerate([(kprev, mprev), (kdiag, Mdiag)]):
                def fn(eng, kb=kb, kap=kap, mk=mk):
                    ps = bank(scb[kb])
                    ins = eng.matmul(ps.rearrange("p (a b) -> p a b", a=4), lhsT=identB,
                                     rhs=mk.unsqueeze(1).to_broadcast([128, 4, 128]),
                                     start=True, stop=False, skip_group_check=True)
                    for h4 in range(4):
                        h = 8 * g + 4 * hh + h4
                        ins = eng.matmul(ps[:, h4 * 128:(h4 + 1) * 128], lhsT=kap[:, h % 2, :],
                                         rhs=qrot[:, h // 2, qc0:qc0 + 128], start=False, stop=True,
                                         skip_group_check=True)
                    return ins
                P.op("pe", fn, reads=prev_res + [("KT", g), "cB"] + [("qrot", (8 * g + 4 * hh) // 2 + i) for i in range(2)],
                     writes=[("ps", scb[kb])])
                act(PT[u][:, kb, :], bank(scb[kb]), AF.Exp, [("ps", scb[kb])], [("PT", u, kb)], scale=0.125)

            def pv():
                def fn(eng):
                    eng.matmul(bank(ob), lhsT=vprev, rhs=PT[u][:, 0, :], start=True, stop=False)
                    eng.matmul(bank(ob), lhsT=vdiag, rhs=PT[u][:, 1, :], start=False, stop=True)
                    eng.matmul(bank(db), lhsT=onesB, rhs=PT[u][:, 0, :], start=True, stop=False)
                    return eng.matmul(bank(db), lhsT=onesB, rhs=PT[u][:, 1, :], start=False, stop=True)
                P.op("pe", fn, reads=prev_res + [("Vd", kcur), ("PT", u, 0), ("PT", u, 1), "cB"],
                     writes=[("ps", ob), ("ps", db)])
                h0 = 8 * g + 4 * hh
                tt(nrm[u].rearrange("p (a b) -> p a b", a=4), bank(db).rearrange("p (a b) -> p a b", a=4),
                   esink[:, h0:h0 + 4].unsqueeze(2).to_broadcast([128, 4, 128]), ALU.add,
                   [("ps", db), "esink"], [("nrm", u)])
                act(nrm[u], nrm[u], AF.Ln, [("nrm", u)], [("nrm", u)])
                act(nrm[u], nrm[u], AF.Exp, [("nrm", u)], [("nrm", u)], scale=-1.0)
                c2 = h0 // 2
                for odd in range(2):
                    b0 = odd * 64
                    tt(ao[b0:b0 + 64, c2:c2 + 2, nq0 + qc0:nq0 + qc0 + 128],
                       bank(ob).rearrange("p (c two q) -> p c two q", c=2, two=2)[b0:b0 + 64, :, odd, :],
                       nrm[u].rearrange("p (c two q) -> p c two q", c=2, two=2)[b0:b0 + 64, :, odd, :],
                       ALU.mult, [("ps", ob), ("nrm", u)], [("ao", c2), ("ao", c2 + 1)])
            if pend_pv:
                pend_pv.pop(0)()
            pend_pv.append(pv)

        for qb in range(nqb):
            for g in range(4):
                for hh in range(2):
                    prompt_unit(qb, g, hh)
        while pend_pv:
            pend_pv.pop(0)()
        qs0 = cs - nq0
        pend_s = []
        for s in range(8):
            u = unit["i"] % 2
            unit["i"] += 1
            k2 = s % 2
            seq = seq0 + s
            scb = [u * 4 + 0, u * 4 + 1]
            ob, db = u * 4 + 2, u * 4 + 3
            sc_c = bank(scb[0], 128); sc_n = bank(scb[1], 128)
            P.dma("pool", "kc%d" % k2, KcT[k2][0:64, :, 0, :], ckT[seq, 0:64], writes=[("KcT", k2)])
            P.dma("pool", "kd%d" % k2, KcT[k2][64:128, :, 1, :], ckT[seq, 64:128], writes=[("KcT", k2)])
            P.dma("pool", "vc%d" % k2, Vcc[k2].rearrange("p a b -> p (a b)"),
                  cv[seq].rearrange("p a b -> p (a b)"), writes=[("Vcc", k2)])

            def fn(eng, s=s, k2=k2, sc_c=sc_c, sc_n=sc_n):
                eng.matmul(sc_c.rearrange("p (a b) -> p a b", a=4), lhsT=identB,
                           rhs=Mc.unsqueeze(1).to_broadcast([128, 4, 32]), start=True, stop=False, skip_group_check=True)
                ins = eng.matmul(sc_n[0:32, :].rearrange("p (a b) -> p a b", a=4), lhsT=identB[0:32, 0:32],
                                 rhs=Mnew[0:32, s * 32:(s + 1) * 32].unsqueeze(1).to_broadcast([32, 4, 32]),
                                 start=True, stop=False, skip_group_check=True)
                for g in range(4):
                    for h8 in range(8):
                        h = 8 * g + h8
                        col = g * 32 + h8 * 4
                        qq = qrot[:, h // 2, qs0 + 4 * s: qs0 + 4 * s + 4]
                        eng.matmul(sc_c[:, col:col + 4], lhsT=KcT[k2][:, g, h % 2, :], rhs=qq,
                                   start=False, stop=True, skip_group_check=True)
                        ins = eng.matmul(sc_n[0:32, col:col + 4], lhsT=KT[:, g, h % 2, cs:cs + 32], rhs=qq,
                                         start=False, stop=True, skip_group_check=True)
                return ins
            P.op("pe", fn, reads=[("KcT", k2), "cB"] + [("KT", g) for g in range(4)] + [("qrot", i) for i in range(16)],
                 writes=[("ps", scb[0]), ("ps", scb[1])])
            PTc = PT[u][:, 0, 0:128]; PTn = PT[u][0:32, 1, 0:128]
            act(PTc, sc_c, AF.Exp, [("ps", scb[0])], [("PT", u, 0)], scale=0.125)
            act(PTn, sc_n[0:32, :], AF.Exp, [("ps", scb[1])], [("PT", u, 1)], scale=0.125)

            def pv_s(s=s, u=u, k2=k2, ob=ob, db=db, PTc=PTc, PTn=PTn):
                def fn(eng, k2=k2, ob=ob, db=db, PTc=PTc, PTn=PTn):
                    for g in range(4):
                        o = bank(ob, 128)[:, g * 32:(g + 1) * 32]
                        eng.matmul(o, lhsT=Vcc[k2][:, g, :], rhs=PTc[:, g * 32:(g + 1) * 32], start=True, stop=False,
                                   skip_group_check=True)
                        eng.matmul(o, lhsT=Vd[0:32, nblk_prompt, g, :], rhs=PTn[:, g * 32:(g + 1) * 32],
                                   start=False, stop=True, skip_group_check=True)
                    eng.matmul(bank(db, 128), lhsT=onesB, rhs=PTc, start=True, stop=False)
                    return eng.matmul(bank(db, 128), lhsT=onesB[0:32, :], rhs=PTn, start=False, stop=True)
                P.op("pe", fn, reads=[("Vcc", k2), ("Vd", nblk_prompt), ("PT", u, 0), ("PT", u, 1), "cB"],
                     writes=[("ps", ob), ("ps", db)])
                nv = nrm[u][:, 0:128]
                tt(nv.rearrange("p (h t) -> p h t", t=4), bank(db, 128).rearrange("p (h t) -> p h t", t=4),
                   esink[:, 0:32].unsqueeze(2).to_broadcast([128, 32, 4]), ALU.add, [("ps", db), "esink"], [("nrm", u)])
                act(nv, nv, AF.Ln, [("nrm", u)], [("nrm", u)])
                act(nv, nv, AF.Exp, [("nrm", u)], [("nrm", u)], scale=-1.0)
                for odd in range(2):
                    b0 = odd * 64
                    src = bank(ob, 128).rearrange("p (c two t) -> p c two t", c=16, two=2)[b0:b0 + 64, :, odd, :]
                    nn = nv.rearrange("p (c two t) -> p c two t", c=16, two=2)[b0:b0 + 64, :, odd, :]
                    dst = ao[b0:b0 + 64, :, cs + 4 * s:cs + 4 * s + 4]
                    tt(dst, src, nn, ALU.mult, [("ps", ob), ("nrm", u)], [("ao", i) for i in range(16)])
            if pend_s:
                pend_s.pop(0)()
            pend_s.append(pv_s)
        while pend_s:
            pend_s.pop(0)()
        stage_done(name + '_attn')
        lastb = nblk_prompt - 1
        P.op("dve", lambda e: e.tensor_copy(out=KTc.rearrange("p a b c -> p (a b) c"),
                                            in_=KT.rearrange("p a b c -> p (a b) c")[:, :, lastb * 128:(lastb + 1) * 128]),
             reads=[("KT", g) for g in range(4)], writes=["KTc"])
        P.op("dve", lambda e: e.tensor_copy(out=Vc_carry, in_=Vd[:, lastb, :, :]), reads=[("Vd", lastb)], writes=["Vcc"])
        P.fence(pool=False)
        RY.reset()
        yT = RY.take(F32, [16, N])
        if nq0 > 0:
            P.op("dve", lambda e: e.memset(yT[:, :, 0:nq0], 0.0), writes=[("yT", c) for c in range(16)])
        proj_to_y(w_o, lambda kc: ao[:, kc, :], lambda kc: ("ao", kc), N, yT, tl=qtl)
        postnorm_add(hT, yT, N, "npost1")
        prenorm(hT, N, xn, "fpre1")
        ffn(1, hT, xn, N, yT, U, tl=qtl)
        for c in range(16):
            P.dma("sp", "yo%d" % (c % 4), yT_o[name][c * 128:(c + 1) * 128, :], hT[:, c, :], reads=[("hT", c)])
        P.fence(pool=False)

    RH.reset(); RXU.reset()
    xnP = RH.take(BF16, [16, NP_])
    stg = RXU.take(F32, [16, NP_ // 2])
    for hf in range(2):
        c0 = hf * (NP_ // 2)
        load_prenorm(xP[:, c0:c0 + NP_ // 2], NP_ // 2, stg, xnP[:, :, c0:c0 + NP_ // 2], "npre0")
    try:
        stage_done("setup")
        stage_done("P_norm")
        hgrn_stage("P", NP_, xnP, False, 7, 0, 0, None)
        P.fence(pool=False)
        stage_done("P")
        run_pass("A", xA, NA_, 5, 128, 0)
        stage_done("A")
        run_pass("B", xB, NB_, 4, 0, 8)
    except StopBuild:
        pass
    P.dma("sp", "spo", sp_out.rearrange("h k v -> k h v"), S_all, reads=["S_all"])
    P.emit()
    es.close()
    return nc


def _host_prep(inp):
    f32 = np.float32
    x_prompt = np.asarray(inp["x_prompt"], f32); x_sample = np.asarray(inp["x_sample"], f32)
    st = np.asarray(inp["state_hgrn"], f32)[0]
    ck = np.asarray(inp["cache_k_win"], f32); cvw = np.asarray(inp["cache_v_win"], f32)
    w_in = np.asarray(inp["hgrn_w_in"], f32)[0].reshape(D, 4, 16, 128)
    w_in_r = np.ascontiguousarray(w_in[:, [0, 3, 1, 2]].transpose(2, 0, 1, 3).reshape(16, D, 512))
    w_gu = np.asarray(inp["ffn_w_gate_up"], f32).reshape(2, D, 2, NJ, 128)
    w_gu_r = np.ascontiguousarray(w_gu.transpose(0, 1, 3, 2, 4).reshape(2, D, NJ * 256))
    wkv = np.asarray(inp["w_kv"], f32).reshape(D, 2, 4, 64)
    wkv_r = np.ascontiguousarray(np.concatenate([wkv, wkv], axis=3).reshape(D, 1024))

    def fm(v):
        return np.asarray(v, f32).reshape(16, 128).T
    vecs = np.stack([fm(inp["norm_mix_pre"][0]), fm(inp["norm_mix_post"][0]), fm(inp["norm_ffn_pre"][0]),
                     fm(inp["norm_ffn_post"][0]), fm(inp["norm_mix_pre"][1]), fm(inp["norm_mix_post"][1]),
                     fm(inp["norm_ffn_pre"][1]), fm(inp["norm_ffn_post"][1]), fm(inp["kv_norm"])], axis=1)
    vecs = np.ascontiguousarray(vecs.reshape(128, 144))
    lbr = np.asarray(inp["hgrn_lower_bounds"], f32)
    lbraw = np.ascontiguousarray(np.concatenate([fm(lbr[0]), fm(lbr[1])], axis=1))
    gnorm = np.ascontiguousarray(fm(inp["hgrn_g_norm"][0]))
    sinks = np.ascontiguousarray(np.broadcast_to(np.asarray(inp["attn_sinks"], f32)[0][None, :], (128, 32)))
    I = np.eye(128, dtype=f32)
    perm = np.zeros((128, 128), f32)
    for m in range(128):
        d = m % 64
        if d < 16:
            perm[m - d + ((d + 8) % 16), m] = 1.0
    cF = np.concatenate([I, np.ones((128, 128), f32), perm], axis=1)
    s_ = np.arange(128)[:, None]; t_ = np.arange(128)[None, :]
    maskBD = ((s_ // 64 == t_ // 64) & (s_ <= t_)).astype(f32)
    Mdiag = np.where(s_ <= t_, 0.0, NEG).astype(f32)
    Mprev = np.where(s_ > t_, 0.0, NEG).astype(f32)
    Mc = np.zeros((128, 32), f32)
    for h8 in range(8):
        for t in range(4):
            Mc[:, h8 * 4 + t] = np.where(np.arange(128) >= t + 1, 0.0, NEG)
    blockmask = np.zeros((128, 8), f32)
    for r in range(32):
        blockmask[r, r // 4] = 1.0
    maskS = np.zeros((128, 32), f32)
    for r in range(32):
        for c in range(32):
            maskS[r, c] = 1.0 if (r // 4 == c // 4 and r <= c) else 0.0
    Mnew = np.full((128, 256), NEG, f32)
    for r in range(32):
        for s in range(8):
            for h8 in range(8):
                for t in range(4):
                    if r // 4 == s and r % 4 <= t:
                        Mnew[r, s * 32 + h8 * 4 + t] = 0.0
    zpad = np.zeros((128, 128), f32)

    def scanmask(n_prompt, n_samp):
        m = np.ones(n_prompt + n_samp, f32)
        m[0:n_prompt:64] = 0.0
        m[n_prompt::4] = 0.0
        return np.ascontiguousarray(np.broadcast_to(m[None, :], (128, m.size)))

    inv = (np.float32(500000.0) ** (-np.arange(0, 16, 2, dtype=f32) / np.float32(16))).astype(f32)

    def rope_tab(pos):
        pos = np.asarray(pos, f32)
        ang = (pos[:, None] * inv[None, :]).astype(f32)
        cos = np.cos(ang).astype(f32); sin = np.sin(ang).astype(f32)
        C = np.ones((128, pos.size), f32); S = np.zeros((128, pos.size), f32)
        for p in range(128):
            d = p % 64
            if d < 16:
                C[p] = cos[:, d % 8]
                S[p] = -sin[:, d % 8] if d < 8 else sin[:, d % 8]
        return np.ascontiguousarray(np.stack([C, S], axis=1))

    shared = dict(w_in=w_in_r, w_out=np.ascontiguousarray(np.asarray(inp["hgrn_w_out"], f32)[0]), w_gu=w_gu_r,
                  w_dn=np.ascontiguousarray(np.asarray(inp["ffn_w_down"], f32)), w_kv=wkv_r,
                  w_q=np.ascontiguousarray(np.asarray(inp["attn_w_q"], f32)[0]),
                  w_o=np.ascontiguousarray(np.asarray(inp["attn_w_out"], f32)[0]),
                  vecs=vecs, lbraw=lbraw, gnorm=gnorm, sinks=sinks, cF=np.ascontiguousarray(cF),
                  scanP=scanmask(NP_, 0), scanA=scanmask(640, 32), scanB=scanmask(512, 32))
    maps = []
    for c in range(8):
        j, half = c // 2, c % 2
        t0 = half * 1024
        xs = x_prompt[j]
        own = xs[t0:t0 + 1024]
        if half == 1:
            halo = xs[t0 - 128:t0]; pref = xs[0:NP_]
            mpf = Mprev
        else:
            halo = np.zeros((128, D), f32); pref = np.zeros((NP_, D), f32)
            mpf = np.full((128, 128), NEG, f32)
        xs_s = x_sample[16 * c:16 * c + 16]
        sA = xs_s[0:8].reshape(32, D); sB = xs_s[8:16].reshape(32, D)
        xA = np.ascontiguousarray(np.concatenate([halo, own[0:512], sA], axis=0).T)
        xB = np.ascontiguousarray(np.concatenate([own[512:1024], sB], axis=0).T)
        cB = np.concatenate([I, np.ones((128, 128), f32), maskBD, Mdiag, Mprev, mpf, Mc, blockmask, maskS, Mnew, zpad],
                            axis=1)
        assert cB.shape[1] == 128 * 7 + 32 + 8 + 32 + 256, cB.shape
        posA = np.concatenate([np.arange(t0 - 128, t0 + 512), 8192 + np.tile(np.arange(4), 8)])
        posB = np.concatenate([np.arange(t0 + 512, t0 + 1024), 8192 + np.tile(np.arange(4), 8)])
        ckc = ck[16 * c:16 * c + 16]
        ckT = ckc.transpose(0, 3, 2, 1)
        ckT = np.ascontiguousarray(np.concatenate([ckT, ckT], axis=1))
        cvc = cvw[16 * c:16 * c + 16]
        cvd = np.ascontiguousarray(np.concatenate([cvc, cvc], axis=3))
        m = dict(shared)
        m.update(xP=np.ascontiguousarray(pref.T), xA=xA, xB=xB, st_in=np.ascontiguousarray(st[16 * c:16 * c + 16]),
                 ckT=ckT, cv=cvd, cB=np.ascontiguousarray(cB), ropeA=rope_tab(posA), ropeB=rope_tab(posB))
        maps.append(m)
    return maps


_NC = None


def kernel(**inputs):
    global _NC
    if _NC is None:
        _NC = build_program()
    maps = _host_prep(inputs)
    res = run_bass_kernel_spmd(_NC, maps, core_ids=list(range(8)))
    R = res.results
    f32 = np.float32
    y_prompt = np.zeros((4, 2048, D), f32); y_sample = np.zeros((128, 4, D), f32)
    S_p = np.zeros((1, 4, 16, 128, 128), f32); S_s = np.zeros((1, 128, 16, 128, 128), f32)
    k_p = np.zeros((4, 128, 4, 64), f32); v_p = np.zeros((4, 128, 4, 64), f32)
    k_s = np.zeros((128, 4, 4, 64), f32); v_s = np.zeros((128, 4, 4, 64), f32)
    for c in range(8):
        j, half = c // 2, c % 2
        t0 = half * 1024
        yA = R[c]["yT_A"].T; yB = R[c]["yT_B"].T
        y_prompt[j, t0:t0 + 512] = yA[128:640]
        y_prompt[j, t0 + 512:t0 + 1024] = yB[0:512]
        y_sample[16 * c:16 * c + 8] = yA[640:672].reshape(8, 4, D)
        y_sample[16 * c + 8:16 * c + 16] = yB[512:544].reshape(8, 4, D)
        S_s[0, 16 * c:16 * c + 16] = R[c]["st_out"]
        kvA = R[c]["kv_A"]; kvB = R[c]["kv_B"]
        k_s[16 * c:16 * c + 8] = kvA[0][:, :, 128:160].transpose(2, 0, 1).reshape(8, 4, 4, 64)
        v_s[16 * c:16 * c + 8] = kvA[1][:, :, 128:160].transpose(2, 0, 1).reshape(8, 4, 4, 64)
        k_s[16 * c + 8:16 * c + 16] = kvB[0][:, :, 128:160].transpose(2, 0, 1).reshape(8, 4, 4, 64)
        v_s[16 * c + 8:16 * c + 16] = kvB[1][:, :, 128:160].transpose(2, 0, 1).reshape(8, 4, 4, 64)
        if half == 1:
            S_p[0, j] = R[c]["sp_out"]
            k_p[j] = kvB[0][:, :, 0:128].transpose(2, 0, 1)
            v_p[j] = kvB[1][:, :, 0:128].transpose(2, 0, 1)
    return (y_prompt, y_sample, S_p, S_s, k_p, v_p, k_s, v_s)
```
